# Optimizing a Trainium2 kernel written in Bass

```python
import math
import jax
import jax.numpy as jnp
from jax import lax
import numpy as np

D_MODEL = 1024
BATCH = 16
SEQ = 2048
DEPTH = 4
DEC_BATCH = 8
DEC_SEQ = 16
PAST_LEN = 2048

CHUNK = 64
N_A = DEPTH // 2
N_B = DEPTH - N_A
HEAD_A = 64
H_A = D_MODEL // HEAD_A
LORA_DECAY = max(32, int(round(1.8 * math.sqrt(D_MODEL) / 32)) * 32)
LORA_A = max(32, int(round(1.8 * math.sqrt(D_MODEL) / 32)) * 32)
LORA_V = max(32, int(round(1.3 * math.sqrt(D_MODEL) / 32)) * 32)
LORA_G = max(32, int(round(0.6 * D_MODEL ** 0.8 / 32)) * 32)
LNX_EPS = 64e-5
H_B = 16
HEAD_B = D_MODEL // H_B
N_PAST_CHUNKS = 8
BAND_PAST = N_PAST_CHUNKS * CHUNK
REL_CLIP = 128
D_FF = -(-(8 * D_MODEL) // (3 * 256)) * 256
RMS_EPS = 1e-6
NEG_INF = -1e30

kernel_name = 'hybrid_rwkv7_chunkband_yoco_step'


def rms_norm(x, g):
    xf = x.astype(jnp.float32)
    y = xf * lax.rsqrt(jnp.mean(xf * xf, axis=-1, keepdims=True) + RMS_EPS)
    return (y * g.astype(jnp.float32)).astype(x.dtype)


def modulate(h, shift, scale):
    return h * (1.0 + scale[:, None, :]) + shift[:, None, :]


def to_heads(t, n_heads):
    return t.reshape(t.shape[0], t.shape[1], n_heads, -1)


def l2_normalize(t):
    tf = t.astype(jnp.float32)
    n = jnp.sqrt(jnp.sum(tf * tf, axis=-1, keepdims=True))
    return (tf / jnp.maximum(n, 1e-12)).astype(t.dtype)


def swiglu(h, w_gate, w_up, w_down):
    return (jax.nn.silu(h @ w_gate) * (h @ w_up)) @ w_down


def wkv7_scan(r, w, k, v, a, b, s0):
    def step(s, inp):
        r_t, w_t, k_t, v_t, a_t, b_t = inp
        sa = jnp.einsum('bhij,bhj->bhi', s, a_t)
        s = s * w_t[:, :, None, :] + sa[..., None] * b_t[:, :, None, :] + v_t[..., None] * k_t[:, :, None, :]
        y = jnp.einsum('bhij,bhj->bhi', s, r_t)
        return s, y
    xs = tuple(jnp.moveaxis(t.astype(jnp.float32), 1, 0) for t in (r, w, k, v, a, b))
    s_last, ys = lax.scan(step, s0.astype(jnp.float32), xs)
    return jnp.moveaxis(ys, 0, 1), s_last


def rwkv7_time_mix(h, shift_prev, wkv_prev, v_first, vres, mix, w_r, w_k, w_v, w_o, w0, w1, w2,
                   a0, a1, a2, g1, g2, k_k, k_a, r_k, ln_w, ln_b):
    bsz, t, d = h.shape
    h_prev = jnp.concatenate([shift_prev[:, None, :].astype(h.dtype), h[:, :-1]], axis=1)
    xx = h_prev - h
    xr = h + xx * mix[0]
    xw = h + xx * mix[1]
    xk = h + xx * mix[2]
    xv = h + xx * mix[3]
    xa = h + xx * mix[4]
    xg = h + xx * mix[5]
    r = xr @ w_r
    w_log = -jax.nn.softplus(-(w0 + jnp.tanh(xw @ w1) @ w2)) - 0.5
    decay = jnp.exp(-jnp.exp(w_log.astype(jnp.float32)))
    k = xk @ w_k
    v = xv @ w_v
    if vres is None:
        v_first = v
    else:
        v0, v1, v2 = vres
        v = v + (v_first - v) * jax.nn.sigmoid(v0 + (xv @ v1) @ v2)
    a_gate = jax.nn.sigmoid(a0 + (xa @ a1) @ a2)
    g = jax.nn.sigmoid(xg @ g1) @ g2
    kk = l2_normalize(to_heads(k * k_k, H_A))
    k = k * (1.0 + (a_gate - 1.0) * k_a)
    rh, kh, vh = to_heads(r, H_A), to_heads(k, H_A), to_heads(v, H_A)
    y, wkv_new = wkv7_scan(rh, to_heads(decay, H_A), kh, vh, -kk, kk * to_heads(a_gate, H_A), wkv_prev)
    mu = jnp.mean(y, axis=-1, keepdims=True)
    var = jnp.mean(jnp.square(y - mu), axis=-1, keepdims=True)
    y = ((y - mu) * lax.rsqrt(var + LNX_EPS)).reshape(bsz, t, d)
    y = y * ln_w.astype(jnp.float32) + ln_b.astype(jnp.float32)
    bonus = jnp.sum((rh * kh * r_k).astype(jnp.float32), axis=-1, keepdims=True) * vh.astype(jnp.float32)
    y = (y + bonus.reshape(bsz, t, d)).astype(h.dtype)
    out = (y * g) @ w_o
    return out, h[:, -1], wkv_new.astype(h.dtype), v_first


def shared_kv(x, c, kv_ada_w, kv_ada_b, kv_norm, w_kv, k_norm):
    bsz, t, _ = x.shape
    shift, scale = jnp.split(jax.nn.silu(c) @ kv_ada_w + kv_ada_b, 2, axis=-1)
    h = modulate(rms_norm(x, kv_norm), shift, scale)
    k, v = jnp.split(h @ w_kv, 2, axis=-1)
    k = rms_norm(k.reshape(bsz, t, H_B, HEAD_B), k_norm)
    v = v.reshape(bsz, t, H_B, HEAD_B)
    return k, v


def band_attention(q, k_all, v_all, pos0, hist_len, rel_table, qb):
    bsz, t, nh, dh = q.shape
    n_blk = t // qb
    band = hist_len + qb
    scale = dh ** -0.5

    def block(i):
        start = i * qb
        q_b = lax.dynamic_slice_in_dim(q, start, qb, axis=1)
        k_b = lax.dynamic_slice_in_dim(k_all, start, band, axis=1)
        v_b = lax.dynamic_slice_in_dim(v_all, start, band, axis=1)
        q_pos = pos0 + start + jnp.arange(qb)
        k_pos = pos0 + start - hist_len + jnp.arange(band)
        rel = jnp.clip(q_pos[:, None] - k_pos[None, :], -REL_CLIP, REL_CLIP) + REL_CLIP
        bias = rel_table[:, rel].astype(jnp.float32)
        q_chunk = q_pos // CHUNK
        k_chunk = k_pos // CHUNK
        allowed = ((k_pos[None, :] >= 0) & (k_chunk[None, :] <= q_chunk[:, None])
                   & (k_chunk[None, :] >= q_chunk[:, None] - N_PAST_CHUNKS))
        s = jnp.einsum('bqhd,bkhd->bhqk', q_b, k_b).astype(jnp.float32) * scale + bias[None]
        s = jnp.where(allowed[None, None], s, NEG_INF)
        p = jax.nn.softmax(s, axis=-1)
        return jnp.einsum('bhqk,bkhd->bqhd', p.astype(v_b.dtype), v_b)

    out = lax.map(block, jnp.arange(n_blk))
    return jnp.moveaxis(out, 0, 1).reshape(bsz, t, nh, dh)


def run_trunk(x, c, shift0, wkv0, k_hist, v_hist, pos0, p):
    bsz, t, d = x.shape
    qb = min(t, CHUNK)
    v_first = None
    shifts = []
    wkvs = []
    k_new = None
    v_new = None
    k_all = None
    v_all = None
    for l in range(DEPTH):
        mod = jax.nn.silu(c) @ p['w_ada'][l] + p['b_ada'][l]
        sh_m, sc_m, gt_m, sh_f, sc_f, gt_f = jnp.split(mod, 6, axis=-1)
        h = modulate(rms_norm(x, p['norm_mix'][l]), sh_m, sc_m)
        if l < N_A:
            vres = None if l == 0 else (p['rw_v0'][l - 1], p['rw_v1'][l - 1], p['rw_v2'][l - 1])
            out, sh, st, v_first = rwkv7_time_mix(
                h, shift0[l], wkv0[l], v_first, vres, p['rw_mix'][l], p['rw_r'][l], p['rw_k'][l],
                p['rw_v'][l], p['rw_o'][l], p['rw_w0'][l], p['rw_w1'][l], p['rw_w2'][l],
                p['rw_a0'][l], p['rw_a1'][l], p['rw_a2'][l], p['rw_g1'][l], p['rw_g2'][l],
                p['rw_kk'][l], p['rw_ka'][l], p['rw_rk'][l], p['rw_lnw'][l], p['rw_lnb'][l])
            shifts.append(sh)
            wkvs.append(st)
        else:
            if l == N_A:
                k_new, v_new = shared_kv(x, c, p['kv_ada_w'], p['kv_ada_b'], p['kv_norm'],
                                         p['w_kv'], p['k_norm'])
                k_all = jnp.concatenate([k_hist.astype(k_new.dtype), k_new], axis=1)
                v_all = jnp.concatenate([v_hist.astype(v_new.dtype), v_new], axis=1)
            j = l - N_A
            q = rms_norm((h @ p['wb_q'][j]).reshape(bsz, t, H_B, HEAD_B), p['q_norm'][j])
            o = band_attention(q, k_all, v_all, pos0, k_hist.shape[1], p['rel_bias'][j], qb)
            out = o.reshape(bsz, t, d) @ p['wb_o'][j]
        x = x + gt_m[:, None, :] * out
        h = modulate(rms_norm(x, p['norm_ffn'][l]), sh_f, sc_f)
        x = x + gt_f[:, None, :] * swiglu(h, p['w_gate'][l], p['w_up'][l], p['w_down'][l])
    return x, jnp.stack(shifts), jnp.stack(wkvs), k_new, v_new


def setup_inputs(seed: int = 0) -> dict:
    key = jax.random.key(seed)
    ks = iter(jax.random.split(key, 64))
    f32 = jnp.float32

    def nrm(shape, scale):
        return jax.random.normal(next(ks), shape, f32) * scale

    def gain(shape):
        return 1.0 + nrm(shape, 0.02)

    def unif(shape, lo, hi):
        return jax.random.uniform(next(ks), shape, f32, lo, hi)

    d = D_MODEL
    inv = d ** -0.5
    rows = min(BAND_PAST, PAST_LEN)
    return {
        'x_prompt': nrm((BATCH, SEQ, d), 1.0),
        'x_sample': nrm((DEC_BATCH, DEC_SEQ, d), 1.0),
        'c_prompt': nrm((BATCH, d), 1.0),
        'c_sample': nrm((DEC_BATCH, d), 1.0),
        'state_shift': nrm((N_A, DEC_BATCH, d), 1.0),
        'state_wkv': nrm((N_A, DEC_BATCH, H_A, HEAD_A, HEAD_A), 0.5),
        'cache_k': nrm((DEC_BATCH, rows, H_B, HEAD_B), 1.0),
        'cache_v': nrm((DEC_BATCH, rows, H_B, HEAD_B), 1.0),
        'w_ada': nrm((DEPTH, d, 6 * d), 0.5 * inv),
        'b_ada': nrm((DEPTH, 6 * d), 0.02),
        'norm_mix': gain((DEPTH, d)),
        'norm_ffn': gain((DEPTH, d)),
        'rw_mix': unif((N_A, 6, d), 0.0, 1.0),
        'rw_r': nrm((N_A, d, d), inv),
        'rw_k': nrm((N_A, d, d), inv),
        'rw_v': nrm((N_A, d, d), inv),
        'rw_o': nrm((N_A, d, d), inv),
        'rw_w0': unif((N_A, d), -6.5, -1.5),
        'rw_w1': nrm((N_A, d, LORA_DECAY), inv),
        'rw_w2': nrm((N_A, LORA_DECAY, d), 0.1 * LORA_DECAY ** -0.5),
        'rw_a0': nrm((N_A, d), 0.1),
        'rw_a1': nrm((N_A, d, LORA_A), inv),
        'rw_a2': nrm((N_A, LORA_A, d), 0.1 * LORA_A ** -0.5),
        'rw_v0': 1.0 + nrm((N_A - 1, d), 0.1),
        'rw_v1': nrm((N_A - 1, d, LORA_V), inv),
        'rw_v2': nrm((N_A - 1, LORA_V, d), 0.1 * LORA_V ** -0.5),
        'rw_g1': nrm((N_A, d, LORA_G), inv),
        'rw_g2': nrm((N_A, LORA_G, d), LORA_G ** -0.5),
        'rw_kk': 0.85 + nrm((N_A, d), 0.05),
        'rw_ka': 1.0 + nrm((N_A, d), 0.05),
        'rw_rk': -0.04 + nrm((N_A, H_A, HEAD_A), 0.02),
        'rw_lnw': gain((N_A, d)),
        'rw_lnb': nrm((N_A, d), 0.02),
        'kv_ada_w': nrm((d, 2 * d), 0.5 * inv),
        'kv_ada_b': nrm((2 * d,), 0.02),
        'kv_norm': gain((d,)),
        'w_kv': nrm((d, 2 * d), inv),
        'k_norm': gain((HEAD_B,)),
        'wb_q': nrm((N_B, d, d), inv),
        'q_norm': gain((N_B, HEAD_B)),
        'rel_bias': nrm((N_B, H_B, 2 * REL_CLIP + 1), 0.5),
        'wb_o': nrm((N_B, d, d), inv),
        'w_gate': nrm((DEPTH, d, D_FF), inv),
        'w_up': nrm((DEPTH, d, D_FF), inv),
        'w_down': nrm((DEPTH, D_FF, d), D_FF ** -0.5),
    }


def reference(x_prompt, x_sample, c_prompt, c_sample, state_shift, state_wkv, cache_k, cache_v,
              w_ada, b_ada, norm_mix, norm_ffn, rw_mix, rw_r, rw_k, rw_v, rw_o, rw_w0, rw_w1, rw_w2,
              rw_a0, rw_a1, rw_a2, rw_v0, rw_v1, rw_v2, rw_g1, rw_g2, rw_kk, rw_ka, rw_rk, rw_lnw,
              rw_lnb, kv_ada_w, kv_ada_b, kv_norm, w_kv, k_norm, wb_q, q_norm, rel_bias, wb_o,
              w_gate, w_up, w_down):
    p = dict(w_ada=w_ada, b_ada=b_ada, norm_mix=norm_mix, norm_ffn=norm_ffn, rw_mix=rw_mix,
             rw_r=rw_r, rw_k=rw_k, rw_v=rw_v, rw_o=rw_o, rw_w0=rw_w0, rw_w1=rw_w1, rw_w2=rw_w2,
             rw_a0=rw_a0, rw_a1=rw_a1, rw_a2=rw_a2, rw_v0=rw_v0, rw_v1=rw_v1, rw_v2=rw_v2,
             rw_g1=rw_g1, rw_g2=rw_g2, rw_kk=rw_kk, rw_ka=rw_ka, rw_rk=rw_rk, rw_lnw=rw_lnw,
             rw_lnb=rw_lnb, kv_ada_w=kv_ada_w, kv_ada_b=kv_ada_b, kv_norm=kv_norm, w_kv=w_kv,
             k_norm=k_norm, wb_q=wb_q, q_norm=q_norm, rel_bias=rel_bias, wb_o=wb_o,
             w_gate=w_gate, w_up=w_up, w_down=w_down)
    b_p = x_prompt.shape[0]
    t_p = x_prompt.shape[1]
    shift0_p = jnp.zeros((N_A, b_p, D_MODEL), x_prompt.dtype)
    wkv0_p = jnp.zeros((N_A, b_p, H_A, HEAD_A, HEAD_A), x_prompt.dtype)
    hist0_p = jnp.zeros((b_p, BAND_PAST, H_B, HEAD_B), x_prompt.dtype)
    y_prompt, shift_p, wkv_p, k_p, v_p = run_trunk(x_prompt, c_prompt, shift0_p, wkv0_p,
                                                   hist0_p, hist0_p, 0, p)
    keep = min(BAND_PAST, t_p)
    k_p = k_p[:, t_p - keep:]
    v_p = v_p[:, t_p - keep:]
    y_sample, shift_s, wkv_s, k_s, v_s = run_trunk(x_sample, c_sample, state_shift, state_wkv,
                                                   cache_k, cache_v, PAST_LEN, p)
    return (y_prompt, y_sample, shift_p, wkv_p, k_p, v_p, shift_s, wkv_s, k_s, v_s)
```

```python
import contextlib
import numpy as np
import concourse.bass as bass
import concourse.mybir as mybir
from concourse.bass_utils import run_bass_kernel_spmd

F32 = mybir.dt.float32
BF16 = mybir.dt.bfloat16
F32R = mybir.dt.float32r
AF = mybir.ActivationFunctionType
ALU = mybir.AluOpType

D = 1024
KC = 8
FF = 2816
FC = 22
NT = 128
NBLK = NT // 128
NL = 4
NEG = -30000.0
C0 = float(np.exp(-0.5))


import os
STOP = int(os.environ.get("K_STOP", "0"))


class _Stop(Exception):
    pass


def ck(k):
    if STOP == k:
        raise _Stop()


class Reg:
    __slots__ = ("w", "rd", "excl")

    def __init__(self):
        self.w = None
        self.rd = {}
        self.excl = False


class Buf:
    def __init__(self, t, nreg):
        self.t = t
        self.regs = [Reg() for _ in range(nreg)]
        self.dkey = None
        self.dcnt = 0

    def rg(self, lo=None, hi=None):
        if lo is None or len(self.regs) == 1:
            return self.regs
        if hi is None:
            return [self.regs[lo]]
        return self.regs[lo:hi]


class Gen:
    def __init__(self, nc, stack, TP, TS):
        self.nc = nc
        self.st = stack
        self.TP = TP
        self.TS = TS
        self.eng = {"pe": nc.tensor, "act": nc.scalar, "dve": nc.vector, "pool": nc.gpsimd, "sp": nc.sync}
        self.semobj = {}
        self.cnt = {}
        self.seen = {n: {} for n in self.eng}
        for n in self.eng:
            self.semobj[n] = stack.enter_context(nc.semaphore("s_" + n))
            self.cnt[n] = 0
        self.ndsem = 0
        self.out_events = {}
        self.evrr = 0

    def sb(self, name, shape, dt, nreg=1):
        t = self.st.enter_context(self.nc.sbuf_tensor(name, list(shape), dt))
        return Buf(t, nreg)

    def ps(self, name, nreg=1):
        t = self.st.enter_context(self.nc.psum_tensor(name, [128, 512], F32))
        b = Buf(t, 1)
        b.regs[0].excl = True
        return b

    def dram(self, name, shape, dt, nreg=1):
        t = self.nc.dram_tensor(name, list(shape), dt, kind="Internal").ap()
        return Buf(t, nreg)

    def _wait(self, en, ev):
        key, val = ev
        if self.seen[en].get(key, 0) >= val:
            return
        self.eng[en].wait_ge(self.semobj[key], val)
        self.seen[en][key] = val

    def _deps(self, en, rd, wr):
        ex = [r for r in rd if r.excl]
        if ex:
            rd = [r for r in rd if not r.excl]
            wr = list(wr) + ex
        evs = []
        for r in rd:
            if r.w is not None:
                evs.append(r.w)
        for r in wr:
            if r.w is not None:
                evs.append(r.w)
            for k, ev in r.rd.items():
                evs.append(ev)
        for ev in evs:
            if en == "pe" and ev[0] == "pe":
                continue
            self._wait(en, ev)

    def _mark(self, ev, rd, wr):
        ex = [r for r in rd if r.excl]
        if ex:
            rd = [r for r in rd if not r.excl]
            wr = list(wr) + ex
        for r in rd:
            r.rd[ev[0]] = ev
        for r in wr:
            r.w = ev
            r.rd = {}

    def op(self, en, fn, rd, wr):
        self._deps(en, rd, wr)
        ins = fn(self.eng[en])
        self.cnt[en] += 1
        ins.then_inc(self.semobj[en], 1)
        self._mark((en, self.cnt[en]), rd, wr)

    def mm(self, items, rd, wr):
        self._deps("pe", rd, wr)
        ins = None
        for (o, l, r, s, e) in items:
            ins = self.nc.tensor.matmul(o, lhsT=l, rhs=r, start=s, stop=e)
        self.cnt["pe"] += 1
        ins.then_inc(self.semobj["pe"], 1)
        self._mark(("pe", self.cnt["pe"]), rd, wr)

    def tr(self, items, rd, wr):
        self._deps("pe", rd, wr)
        ins = None
        for (o, i, idn) in items:
            ins = self.nc.tensor.transpose(out=o, in_=i, identity=idn)
        self.cnt["pe"] += 1
        ins.then_inc(self.semobj["pe"], 1)
        self._mark(("pe", self.cnt["pe"]), rd, wr)

    def dma(self, buf, out, in_, rd, wr, is_out=False, slow=False):
        en = "sp"
        if buf.dkey is None:
            buf.dkey = "d%d" % self.ndsem
            self.ndsem += 1
            self.semobj[buf.dkey] = self.st.enter_context(self.nc.semaphore(buf.dkey))
        self._deps(en, rd, wr)
        if buf.dcnt > 0:
            self._wait(en, (buf.dkey, buf.dcnt))
        if slow:
            ins = self.nc.sync.dma_start(out=out, in_=in_, allow_slow_non_contiguous=True)
        else:
            ins = self.nc.sync.dma_start(out=out, in_=in_)
        buf.dcnt += 16
        ins.then_inc(self.semobj[buf.dkey], 16)
        ev = (buf.dkey, buf.dcnt)
        self._mark(ev, rd, wr)
        if is_out:
            self.out_events[buf.dkey] = buf.dcnt

    def finish(self):
        for k, v in self.out_events.items():
            self._wait("sp", (k, v))

    def ev_eng(self):
        self.evrr ^= 1
        return "act" if self.evrr else "dve"

    def copy(self, en, out, in_, rd, wr):
        if en == "act":
            self.op("act", lambda e: e.activation(out=out, in_=in_, func=AF.Copy), rd, wr)
        else:
            self.op(en, lambda e: e.tensor_copy(out=out, in_=in_), rd, wr)


def build(TP, TS):
    nc = bass.Bass("TRN2", target_bir_lowering=False)
    st = contextlib.ExitStack()
    g = Gen(nc, st, TP, TS)
    try:
        _build(nc, st, g, TP, TS)
    except _Stop:
        pass
    g.finish()
    return nc, st


def _build(nc, st, g, TP, TS):
    NBP = 2

    def din(name, shape):
        return nc.dram_tensor(name, list(shape), F32, kind="ExternalInput").ap()

    def dout(name, shape):
        return nc.dram_tensor(name, list(shape), F32, kind="ExternalOutput").ap()

    KEEP = min(512, TP)
    I = dict(
        x_prompt=din("x_prompt", [NBP, TP, D]), x_sample=din("x_sample", [1, TS, D]),
        c_prompt=din("c_prompt", [NBP, D]), c_sample=din("c_sample", [1, D]),
        state_shift=din("state_shift", [2, 1, D]), state_wkv=din("state_wkv", [2, 1, 16, 64, 64]),
        cache_k=din("cache_k", [1, 512, D]), cache_v=din("cache_v", [1, 512, D]),
        w_ada=din("w_ada", [4, D, 6 * D]), b_ada=din("b_ada", [4, 6 * D]),
        norm_mix=din("norm_mix", [4, D]), norm_ffn=din("norm_ffn", [4, D]),
        rw_mix=din("rw_mix", [2, 6, D]), rw_r=din("rw_r", [2, D, D]), rw_k=din("rw_k", [2, D, D]),
        rw_v=din("rw_v", [2, D, D]), rw_o=din("rw_o", [2, D, D]), rw_w0=din("rw_w0", [2, D]),
        rw_w1=din("rw_w1", [2, D, 64]), rw_w2=din("rw_w2", [2, 64, D]), rw_a0=din("rw_a0", [2, D]),
        rw_a1=din("rw_a1", [2, D, 64]), rw_a2=din("rw_a2", [2, 64, D]), rw_v0=din("rw_v0", [1, D]),
        rw_v1=din("rw_v1", [1, D, 32]), rw_v2=din("rw_v2", [1, 32, D]), rw_g1=din("rw_g1", [2, D, 160]),
        rw_g2=din("rw_g2", [2, 160, D]), rw_kk=din("rw_kk", [2, D]), rw_ka=din("rw_ka", [2, D]),
        rw_rk=din("rw_rk", [2, D]), rw_lnw=din("rw_lnw", [2, D]), rw_lnb=din("rw_lnb", [2, D]),
        kv_ada_w=din("kv_ada_w", [D, 2 * D]), kv_ada_b=din("kv_ada_b", [2 * D]), kv_norm=din("kv_norm", [D]),
        w_kv=din("w_kv", [D, 2 * D]), k_norm=din("k_norm", [64]), wb_q=din("wb_q", [2, D, D]),
        q_norm=din("q_norm", [2, 64]), rel_bias=din("rel_bias", [2, 16, 257]), wb_o=din("wb_o", [2, D, D]),
        w_gate=din("w_gate", [4, D, FF]), w_up=din("w_up", [4, D, FF]), w_down=din("w_down", [4, FF, D]),
    )
    O = dict(
        y_prompt=dout("y_prompt", [NBP, TP, D]), y_sample=dout("y_sample", [1, TS, D]),
        shift_p=dout("shift_p", [2, NBP, D]), wkv_p=dout("wkv_p", [2, NBP, 16, 64, 64]),
        k_p=dout("k_p", [NBP, KEEP, D]), v_p=dout("v_p", [NBP, KEEP, D]),
        shift_s=dout("shift_s", [2, 1, D]), wkv_s=dout("wkv_s", [2, 1, 16, 64, 64]),
        k_s=dout("k_s", [1, TS, D]), v_s=dout("v_s", [1, TS, D]),
    )

    ident = g.sb("ident", [128, 128], F32)
    identb = g.sb("identb", [128, 128], BF16)
    onesm = g.sb("onesm", [128, 128], BF16)
    onesb = g.sb("onesb", [128, 128], BF16)
    onesb64 = g.sb("onesb64", [128, 128], BF16)
    ones1 = g.sb("ones1", [128, 128], BF16)
    m_sl = g.sb("m_sl", [128, 128], F32)
    m_su = g.sb("m_su", [128, 128], F32)
    m_iu = g.sb("m_iu", [128, 128], F32)
    epsb = g.sb("epsb", [128, 4], F32)
    NV = 80
    rows = g.sb("rows", [NV, D], F32)
    pv = g.sb("pv", [128, NV, 8], F32)
    scT = g.sb("scT", [128, 8, 4], F32)
    mod = g.sb("mod", [128, NL, 48, 3], F32)
    kvm = g.sb("kvm", [128, 16, 3], F32)
    gm = g.sb("gm", [128, NL * 2 * 3, 8], F32)
    gkv = g.sb("gkv", [128, 3, 8], F32)

    PS = [g.ps("ps%d" % i) for i in range(8)]

    xin = g.sb("xin", [128, NBLK, D], F32, NBLK)
    xT = g.sb("xT", [128, KC, NT], F32, KC)
    hb = g.sb("hb", [128, KC, NT], F32, KC)
    xx = g.sb("xx", [128, KC, NT], F32, KC)
    hbf = g.sb("hbf", [128, KC, NT], BF16, KC)
    xm = [g.sb("xm%d" % i, [128, KC, NT], BF16, KC) for i in range(2)]
    rT = g.sb("rT", [128, KC, NT], BF16, KC)
    kT = g.sb("kT", [128, KC, NT], BF16, KC)
    bT = g.sb("bT", [128, KC, NT], BF16, KC)
    aT = g.sb("aT", [128, KC, NT], BF16, KC)
    aT2 = g.sb("aT2", [128, KC, 2, NT], BF16, KC)
    bT2 = g.sb("bT2", [128, KC, 2, NT], BF16, KC)
    rT2 = g.sb("rT2", [128, KC, 2, NT], BF16, KC)
    vtm = g.sb("vtm", [128, NBLK, D], BF16, NBLK)
    ktm = g.sb("ktm", [128, NBLK, D], BF16, NBLK)
    btm = g.sb("btm", [128, NBLK, D], BF16, NBLK)
    vfirst = g.sb("vfirst", [128, KC, NT], F32, KC)
    T1 = g.sb("T1", [128, KC, NT], F32, KC)
    T2 = g.sb("T2", [128, KC, NT], F32, KC)
    decbufs = [g.sb("decb%d" % i, [128, KC, NT], F32, KC) for i in range(3)]
    agbuf = g.sb("agbuf", [128, KC, NT], F32, KC)
    kfbuf = g.sb("kfbuf", [128, KC, NT], F32, KC)
    bonus = decbufs[2]
    gate = agbuf
    wc = g.sb("wc", [128, KC, 2], F32, KC)
    yT = hb
    gact = g.sb("gact", [128, FC, NT], BF16, FC)
    carry = [g.sb("carry%d" % l, [128, KC], F32) for l in range(2)]
    Sf = [g.sb("Sf%d" % l, [128, KC, 64], F32) for l in range(2)]
    Sb = [g.sb("Sb%d" % l, [128, KC, 128], BF16) for l in range(2)]
    sst = g.sb("sst", [64, 16, 64], F32)
    TMP = [g.sb("tmp%d" % i, [128, max(NT, 128)], F32) for i in range(11)]
    tmpi = [0]

    def tmp():
        tmpi[0] = (tmpi[0] + 1) % len(TMP)
        return TMP[tmpi[0]]

    TB = [g.sb("tb%d" % i, [128, NT], BF16) for i in range(8)]
    tbi = [0]

    def tmpb():
        tbi[0] = (tbi[0] + 1) % len(TB)
        return TB[tbi[0]]

    XR = [g.sb("XR%d" % i, [128, 4, 192], F32, 2) for i in range(2)]
    XTb = [g.sb("XTb%d" % i, [128, 4, 128], F32, 2) for i in range(2)]
    AKTb = g.sb("AKTb", [128, 4, 128], BF16)
    ARBb = g.sb("ARBb", [128, 4, 128], BF16)
    ARKb = g.sb("ARKb", [128, 4, 128], BF16)
    Ub = g.sb("Ub", [128, 16, 64], BF16, 4)
    NSLOT = 6
    Kring = g.sb("Kring", [128, KC, NSLOT * 128], BF16, NSLOT)
    Vring = g.sb("Vring", [128, NSLOT, D], BF16, NSLOT)
    qT = rT
    oT = kT
    kvst = g.sb("kvst", [128, NBLK, D], F32, NBLK)
    PT = [g.sb("PT%d" % i, [128, 5, 128], BF16) for i in range(2)]
    biasb = [g.sb("biasb%d" % i, [128, 5, 128], BF16) for i in range(2)]
    BIASE = g.dram("BIASE", [32, 128, 768], F32)
    BIASB = g.dram("BIASB", [32, 128, 5 * 128], BF16, 32)
    NWS, NWB, WPC = 5, 5, 704
    WST = [g.sb("wst%d" % i, [128, WPC], F32) for i in range(NWS)]
    WBF = [g.sb("wbf%d" % i, [128, 2816], BF16, 8) for i in range(NWB)]
    wi = [0]
    wsi = [0]
    biasf = Buf(T1.t[:, 0:5, :], 1)
    biash = Buf(T2.t[:, 0:5, :].bitcast(BF16)[:, :, 0:128], 1)
    erow = Buf(decbufs[0].t[0:32, :, :].rearrange("p a b -> p (a b)")[:, 0:768], 1)
    biasf.regs = T1.regs
    biash.regs = T2.regs
    erow.regs = decbufs[0].regs
    WSCR_N = 52 * 1024 * 1024
    WSCR = nc.dram_tensor("WSCR", [WSCR_N], BF16, kind="Internal").ap()
    wmemo = {}
    wpend = []
    wofs = [0]

    def V(name):
        return VIDX[name]

    def pcol(row, c):
        return pv.t[:, row, c:c + 1]

    def wslab(src3):
        P, a, b = src3.shape
        key = (src3.tensor.name, int(src3.offset), tuple(tuple(x) for x in src3.ap))
        if key in wmemo and wpend:
            wpend.pop()()
        bf = WBF[wi[0] % NWB]
        wi[0] += 1
        dstb = bf.t[0:P, 0:a * b].rearrange("p (a b) -> p a b", a=a)
        if key in wmemo:
            scr, reg = wmemo[key]
            g.dma(bf, bf.t[0:P, 0:a * b], scr, [reg], bf.rg())
            return dstb, bf.rg()
        if a > 1:
            sa = max(1, WPC // b)
            assert b <= WPC
            pieces = [(a0, min(a, a0 + sa), 0, b) for a0 in range(0, a, sa)]
        else:
            pieces = [(0, 1, b0, min(b, b0 + WPC)) for b0 in range(0, b, WPC)]
        for pidx, (a0, a1, b0, b1) in enumerate(pieces):
            stg = WST[wsi[0] % NWS]
            cen = ("dve", "act", "pool", "dve")[wsi[0] % 4]
            wsi[0] += 1
            dst = stg.t[0:P, 0:(a1 - a0) * (b1 - b0)].rearrange("p (a b) -> p a b", a=a1 - a0)
            g.dma(stg, dst, src3[:, a0:a1, b0:b1], [], stg.rg())
            g.copy(cen, dstb[:, a0:a1, b0:b1], dst, stg.rg(), bf.rg(min(pidx, 7)))
        scr = bass.AP(WSCR.tensor, wofs[0], [[a * b, P], [1, a * b]])
        wofs[0] += P * a * b
        assert wofs[0] <= WSCR_N
        reg = Reg()
        if wpend:
            wpend.pop()()
        wpend.append(lambda: g.dma(bf, scr, bf.t[0:P, 0:a * b], bf.rg(), [reg]))
        wmemo[key] = (scr, reg)
        return dstb, bf.rg()

    psr = [0]

    def pregion(n):
        psr[0] = (psr[0] + 1) % 4
        b = PS[4 + psr[0]]
        return b.t[:, 0:n], b.rg()

    def proj(W, xs, n, sink, M=None):
        K, Mw = W.shape
        M = Mw
        full = all(nr == 128 for (_, _, nr) in xs)
        if full:
            npc = len(xs)
            mper = max(1, min((2816 // npc) // 128, (M + 127) // 128))
            if M < 128:
                slabs = [(0, M)]
            else:
                slabs = [(m0, min(mper * 128, M - m0)) for m0 in range(0, M, mper * 128)]
            for (m0, mc) in slabs:
                src = W[:, m0:m0 + mc].rearrange("(kc p) m -> p kc m", p=128)
                wb, wregs = wslab(src)
                for mo in range(0, mc, 128):
                    mw = min(128, mc - mo)
                    pa, pregs = pregion(n)
                    items = []
                    rd = list(wregs)
                    for ki, (xa, xr, nr) in enumerate(xs):
                        items.append((pa[0:mw, :], wb[:, ki, mo:mo + mw], xa, ki == 0, ki == npc - 1))
                        rd += xr
                    g.mm(items, rd, pregs)
                    sink((m0 + mo) // 128, mw, pa, pregs)
        else:
            wbs = []
            r0 = 0
            for (xa, xr, nr) in xs:
                src = W[r0:r0 + nr, :].rearrange("p (a m) -> p a m", a=1)
                wbs.append(wslab(src))
                r0 += nr
            for m0 in range(0, M, 128):
                mw = min(128, M - m0)
                pa, pregs = pregion(n)
                items = []
                rd = []
                for ki, (xa, xr, nr) in enumerate(xs):
                    wb, wregs = wbs[ki]
                    items.append((pa[0:mw, :], wb[:, 0, m0:m0 + mw], xa, ki == 0, ki == len(xs) - 1))
                    rd += xr + wregs
                g.mm(items, rd, pregs)
                sink(m0 // 128, mw, pa, pregs)

    def xs_of(buf, n):
        return [(buf.t[:, c, 0:n], buf.rg(c), 128) for c in range(KC)]

    e = None
    g.op("pool", lambda e: e.memset(ident.t[:], 1.0), [], ident.rg())
    g.op("pool", lambda e: e.affine_select(out=ident.t[:], in_=ident.t[:], pattern=[[-1, 128]], compare_op=ALU.is_equal,
                                           fill=0.0, base=0, channel_multiplier=1), ident.rg(), ident.rg())
    g.op("dve", lambda e: e.tensor_copy(out=identb.t[:], in_=ident.t[:]), ident.rg(), identb.rg())
    g.op("dve", lambda e: e.memset(onesm.t[:], 1.0 / 1024.0), [], onesm.rg())
    g.op("dve", lambda e: e.memset(ones1.t[:], 1.0), [], ones1.rg())
    for (ob, val) in ((onesb, 1.0), (onesb64, 1.0 / 64.0)):
        g.op("dve", lambda e: e.memset(ob.t[:], 0.0), [], ob.rg())
        g.op("dve", lambda e: e.memset(ob.t[0:64, 0:64], val), [], ob.rg())
        g.op("dve", lambda e: e.memset(ob.t[64:128, 64:128], val), [], ob.rg())
    for (mb, pat, cm, cop) in ((m_sl, -1, 1, ALU.is_gt), (m_su, 1, -1, ALU.is_gt), (m_iu, 1, -1, ALU.is_ge)):
        g.op("pool", lambda e: e.memset(mb.t[:], 1.0), [], mb.rg())
        g.op("pool", lambda e: e.affine_select(out=mb.t[:], in_=mb.t[:], pattern=[[pat, 128]], compare_op=cop,
                                               fill=0.0, base=0, channel_multiplier=cm), mb.rg(), mb.rg())
    g.op("dve", lambda e: e.memset(epsb.t[:, 0:1], 1e-6), [], epsb.rg())
    g.op("dve", lambda e: e.memset(epsb.t[:, 1:2], 64e-5), [], epsb.rg())
    g.op("dve", lambda e: e.memset(epsb.t[:, 2:3], 1e-24), [], epsb.rg())
    g.op("dve", lambda e: e.memset(epsb.t[:, 3:4], 0.0), [], epsb.rg())
    for l in range(2):
        g.op("dve", lambda e: e.memset(Sb[l].t[:], 0.0), [], Sb[l].rg())
    for eb in (aT2, bT2, rT2):
        g.op("pool", lambda e: e.memset(eb.t[:], 0.0), [], eb.rg())
    for eb in (XR[0], XR[1]):
        g.op("pool", lambda e: e.memset(eb.t[:], 0.0), [], eb.rg())

    def expand(src, dst, mi, n):
        g.op("pool", lambda e: e.tensor_copy(out=dst.t[0:64, mi, 0, 0:n], in_=src.t[0:64, mi, 0:n]), src.rg(mi), dst.rg(mi))
        g.op("pool", lambda e: e.tensor_copy(out=dst.t[64:128, mi, 1, 0:n], in_=src.t[64:128, mi, 0:n]), src.rg(mi), dst.rg(mi))

    VIDX = {}
    rp = [0]

    def addrows(name, ap2, n):
        VIDX[name] = rp[0]
        g.dma(rows, rows.t[rp[0]:rp[0] + n, :], ap2, [], rows.rg())
        rp[0] += n

    g.op("dve", lambda e: e.memset(rows.t[:], 0.0), [], rows.rg())
    addrows("norm_mix", I["norm_mix"], 4)
    addrows("norm_ffn", I["norm_ffn"], 4)
    addrows("rw_mix", I["rw_mix"].rearrange("l s d -> (l s) d"), 12)
    for nm in ("rw_w0", "rw_a0", "rw_kk", "rw_ka", "rw_rk", "rw_lnw", "rw_lnb"):
        addrows(nm, I[nm], 2)
    addrows("rw_v0", I["rw_v0"], 1)
    addrows("kv_norm", I["kv_norm"].rearrange("(a d) -> a d", a=1), 1)
    addrows("b_ada", I["b_ada"].rearrange("l (s d) -> (l s) d", s=6), 24)
    addrows("kv_ada_b", I["kv_ada_b"].rearrange("(s d) -> s d", s=2), 2)
    addrows("c_prompt", I["c_prompt"], 2)
    addrows("c_sample", I["c_sample"], 1)
    addrows("state_shift", I["state_shift"].rearrange("l b d -> (l b) d"), 2)
    VIDX["k_norm"] = rp[0]
    kn = I["k_norm"]
    g.dma(rows, rows.t[rp[0]:rp[0] + 1, :].rearrange("p (h j) -> p h j", h=16),
          bass.AP(kn.tensor, 0, [[0, 1], [0, 16], [1, 64]]), [], rows.rg())
    rp[0] += 1
    VIDX["q_norm"] = rp[0]
    qn = I["q_norm"]
    g.dma(rows, rows.t[rp[0]:rp[0] + 2, :].rearrange("p (h j) -> p h j", h=16),
          bass.AP(qn.tensor, 0, [[64, 2], [0, 16], [1, 64]]), [], rows.rg())
    rp[0] += 2
    assert rp[0] <= NV
    for c in range(KC):
        pb = PS[6 + c % 2]
        g.tr([(pb.t[:, 0:NV], rows.t[0:NV, c * 128:(c + 1) * 128], ident.t[0:NV, 0:NV])], rows.rg() + ident.rg(), pb.rg(0, 1))
        g.op("dve", lambda e: e.tensor_copy(out=pv.t[:, :, c], in_=pb.t[:, 0:NV]), pb.rg(0, 1), pv.rg())
    ck(1)
    cr = V("c_prompt")
    g.op("act", lambda e: e.activation(out=scT.t[:, :, 0:3], in_=pv.t[:, cr:cr + 3, :].rearrange("p s c -> p c s"),
                                       func=AF.Silu), pv.rg(), scT.rg())

    rb = I["rel_bias"].rearrange("j h r -> (j h) r")
    g.dma(erow, erow.t[:, 0:256], rb[:, 1:257], [], erow.rg())
    g.op("dve", lambda e: e.tensor_copy(out=erow.t[:, 256:768], in_=erow.t[:, 255:256].to_broadcast([32, 512])),
         erow.rg(), erow.rg())
    g.dma(erow, BIASE.t[:, :, :], erow.t[:, :].unsqueeze(1).to_broadcast([32, 128, 768]), erow.rg(), BIASE.rg())

    def bias_steps():
        for jh in range(32):
            src = bass.AP(BIASE.t.tensor, jh * 128 * 768 + 127, [[767, 128], [128, 5], [1, 128]])
            g.dma(biasf, biasf.t[:], src, BIASE.rg(), biasf.rg())
            g.op("dve", lambda e: e.memset(biasf.t[0:64, 4, 64:128], NEG), biasf.rg(), biasf.rg())
            g.op("dve", lambda e: e.memset(biasf.t[64:128, 0, 0:64], NEG), biasf.rg(), biasf.rg())
            g.op("dve", lambda e: e.tensor_copy(out=biash.t[:], in_=biasf.t[:]), biasf.rg(), biash.rg())
            g.dma(biash, BIASB.t[jh].rearrange("p (t q) -> p t q", t=5), biash.t[:], biash.rg(), BIASB.rg(jh))
            yield
    bias_gen = bias_steps()

    adacnt = [0]
    ADAST = [decbufs[1], decbufs[2], agbuf, kfbuf, hb, xx, vfirst]

    def ada(Wl, ncols, outv, brow):
        for j in range(ncols // 1024):
            for hf in range(2):
                pb = PS[6 + hf]
                for q in range(4):
                    m = j * 8 + hf * 4 + q
                    stg = ADAST[adacnt[0] % len(ADAST)]
                    adacnt[0] += 1
                    g.dma(stg, stg.t[:, :, 0:128], Wl[:, m * 128:(m + 1) * 128].rearrange("(kc p) m -> p kc m", p=128), [], stg.rg())
                    items = [(pb.t[0:3, q * 128:(q + 1) * 128], scT.t[:, kc, 0:3], stg.t[:, kc, 0:128], kc == 0, kc == KC - 1)
                             for kc in range(KC)]
                    g.mm(items, stg.rg() + scT.rg(), pb.rg())
                    if adacnt[0] % 6 == 5:
                        next(bias_gen, None)
                g.copy("act", xin.t[0:3, 0, hf * 512:(hf + 1) * 512], pb.t[0:3, :], pb.rg(), xin.rg(0))
            pt_ = PS[5]
            g.tr([(pt_.t[:, c * 3:(c + 1) * 3], xin.t[0:3, 0, c * 128:(c + 1) * 128], ident.t[0:3, 0:3]) for c in range(KC)],
                 xin.rg(0) + ident.rg(), pt_.rg())
            g.op("dve", lambda e: e.tensor_tensor(out=outv[:, j * 8:(j + 1) * 8, :], in0=pt_.t[:, 0:24].rearrange("p (c s) -> p c s", s=3),
                                                  in1=pv.t[:, brow + j, :].unsqueeze(2).to_broadcast([128, KC, 3]), op=ALU.add),
                 pt_.rg() + pv.rg(), mod.rg() + kvm.rg())

    for l in range(NL):
        ada(I["w_ada"][l], 6 * D, mod.t[:, l], V("b_ada") + 6 * l)
    ada(I["kv_ada_w"], 2 * D, kvm.t[:], V("kv_ada_b"))
    for _ in bias_gen:
        pass
    for l in range(NL):
        for sub in range(2):
            j = 1 if sub == 0 else 4
            grow = (V("norm_mix") if sub == 0 else V("norm_ffn")) + l
            for s in range(3):
                g.op("dve", lambda e: e.scalar_tensor_tensor(
                    out=gm.t[:, (l * 2 + sub) * 3 + s, :], in0=mod.t[:, l, j * 8:j * 8 + 8, s], scalar=1.0,
                    in1=pv.t[:, grow, :], op0=ALU.add, op1=ALU.mult), mod.rg() + pv.rg(), gm.rg())
    for s in range(3):
        g.op("dve", lambda e: e.scalar_tensor_tensor(
            out=gkv.t[:, s, :], in0=kvm.t[:, 8:16, s], scalar=1.0, in1=pv.t[:, V("kv_norm"), :],
            op0=ALU.add, op1=ALU.mult), kvm.rg() + pv.rg(), gkv.rg())

    ck(2)
    ck(3)
    def load_xT(src_tok, n, dstbuf, dst_dt_regs=None):
        nb = (n + 127) // 128
        for b in range(nb):
            nn = min(128, n - b * 128)
            g.dma(xin, xin.t[0:nn, b, :], src_tok[b * 128:b * 128 + nn, :], [], xin.rg(b))
        ck(32)
        for c in range(KC):
            pa, pregs = pregion(n)
            items = []
            for b in range(nb):
                nn = min(128, n - b * 128)
                items.append((pa[:, b * 128:b * 128 + nn], xin.t[0:nn, b, c * 128:(c + 1) * 128], ident.t[0:nn, 0:nn]))
            g.tr(items, xin.rg() + ident.rg(), pregs)
            ck(33)
            g.copy(g.ev_eng(), dstbuf.t[:, c, 0:n], pa[:, 0:n], pregs, dstbuf.rg(c))
            ck(34)

    def store_tok(srcbuf, n, dst_tok, colofs=0, is_out=True, src_ap=None):
        nb = (n + 127) // 128
        for b in range(nb):
            nn = min(128, n - b * 128)
            for half in range(2):
                pb = PS[6 + half]
                items = []
                for cc in range(4):
                    c = half * 4 + cc
                    items.append((pb.t[0:nn, cc * 128:(cc + 1) * 128],
                                  srcbuf.t[:, c, colofs + b * 128:colofs + b * 128 + nn], ident.t[:, :]))
                g.tr(items, srcbuf.rg() + ident.rg(), pb.rg())
                g.copy(g.ev_eng(), xin.t[0:nn, b, half * 512:(half + 1) * 512], pb.t[0:nn, :], pb.rg(), xin.rg(b))
            g.dma(xin, dst_tok[b * 128:b * 128 + nn, :], xin.t[0:nn, b, :], xin.rg(b), [], is_out=is_out)

    def rmsnorm_mod(src, n, gmrow_ap, shift_ap_fn, out_f32=None, out_bf=None):
        sq = tmpb_big[0]
        g.op("act", lambda e: e.activation(out=sq.t[:, :, 0:n], in_=src.t[:, :, 0:n], func=AF.Square), src.rg(), sq.rg())
        pa, pregs = pregion(n)
        g.mm([(pa, onesm.t[:, :], sq.t[:, c, 0:n], c == 0, c == KC - 1) for c in range(KC)], sq.rg() + onesm.rg(), pregs)
        rs = tmp()
        g.op("act", lambda e: e.activation(out=rs.t[:, 0:n], in_=pa, func=AF.Ln, bias=epsb.t[:, 0:1], scale=1.0),
             pregs + epsb.rg(), rs.rg())
        g.op("act", lambda e: e.activation(out=rs.t[:, 0:n], in_=rs.t[:, 0:n], func=AF.Exp, scale=-0.5), rs.rg(), rs.rg())
        for c in range(KC):
            t1 = tmp()
            g.op("dve", lambda e: e.scalar_tensor_tensor(out=t1.t[:, 0:n], in0=src.t[:, c, 0:n], scalar=gmrow_ap[:, c:c + 1],
                                                         in1=rs.t[:, 0:n], op0=ALU.mult, op1=ALU.mult),
                 src.rg(c) + rs.rg() + gm.rg() + gkv.rg(), t1.rg())
            sh = shift_ap_fn(c)
            if out_f32 is not None:
                g.op("act", lambda e: e.activation(out=out_f32.t[:, c, 0:n], in_=t1.t[:, 0:n], func=AF.Identity, bias=sh, scale=1.0),
                     t1.rg() + mod.rg() + kvm.rg(), out_f32.rg(c))
            if out_bf is not None:
                g.op("act", lambda e: e.activation(out=out_bf.t[:, c, 0:n], in_=t1.t[:, 0:n], func=AF.Identity, bias=sh, scale=1.0),
                     t1.rg() + mod.rg() + kvm.rg(), out_bf.rg(c))

    tmpb_big = [aT]

    def ffn(l, s, n):
        gmrow = gm.t[:, (l * 2 + 1) * 3 + s, :]
        rmsnorm_mod(xT, n, gmrow, lambda c: mod.t[:, l, 3 * 8 + c, s:s + 1], out_bf=hbf)
        xs = xs_of(hbf, n)
        gts = {}

        def sink_gate(mi, mw, pa, pregs):
            t = tmp()
            g.op("act", lambda e: e.activation(out=t.t[:, 0:n], in_=pa, func=AF.Exp, scale=-1.0), pregs, t.rg())
            g.op("dve", lambda e: e.tensor_scalar(out=t.t[:, 0:n], in0=t.t[:, 0:n], scalar1=1.0, scalar2=None, op0=ALU.add), t.rg(), t.rg())
            g.op("dve", lambda e: e.reciprocal(out=t.t[:, 0:n], in_=t.t[:, 0:n]), t.rg(), t.rg())
            g.op("dve", lambda e: e.tensor_tensor(out=t.t[:, 0:n], in0=pa, in1=t.t[:, 0:n], op=ALU.mult), pregs + t.rg(), t.rg())
            gts[mi] = t

        def sink_up(mi, mw, pa, pregs):
            t = gts[mi]
            g.op("dve", lambda e: e.tensor_tensor(out=gact.t[:, mi, 0:n], in0=pa, in1=t.t[:, 0:n], op=ALU.mult),
                 pregs + t.rg(), gact.rg(mi))

        Wg, Wu = I["w_gate"][l], I["w_up"][l]
        for m0 in range(0, FF, 256):
            proj(Wg[:, m0:m0 + 256], xs, n, lambda mi, mw, pa, pr, m0=m0: sink_gate(m0 // 128 + mi, mw, pa, pr))
            proj(Wu[:, m0:m0 + 256], xs, n, lambda mi, mw, pa, pr, m0=m0: sink_up(m0 // 128 + mi, mw, pa, pr))
        xs2 = [(gact.t[:, c, 0:n], gact.rg(c), 128) for c in range(FC)]

        def sink_down(mi, mw, pa, pregs):
            g.op("dve", lambda e: e.scalar_tensor_tensor(out=xT.t[:, mi, 0:n], in0=pa, scalar=mod.t[:, l, 5 * 8 + mi, s:s + 1],
                                                         in1=xT.t[:, mi, 0:n], op0=ALU.mult, op1=ALU.add),
                 pregs + xT.rg(mi) + mod.rg(), xT.rg(mi))

        proj(I["w_down"][l], xs2, n, sink_down)

    def resid_sink(l, s, n):
        def sink(mi, mw, pa, pregs):
            g.op("dve", lambda e: e.scalar_tensor_tensor(out=xT.t[:, mi, 0:n], in0=pa, scalar=mod.t[:, l, 2 * 8 + mi, s:s + 1],
                                                         in1=xT.t[:, mi, 0:n], op0=ALU.mult, op1=ALU.add),
                 pregs + xT.rg(mi) + mod.rg(), xT.rg(mi))
        return sink

    def to_tm(src_bf_ap, src_regs, n, dstbuf, c):
        nb = (n + 127) // 128
        pa, pregs = pregion(128 * nb)
        pab = pa.bitcast(BF16)
        items = []
        for b in range(nb):
            nn = min(128, n - b * 128)
            items.append((pab[0:nn, b * 128:(b + 1) * 128], src_bf_ap[:, b * 128:b * 128 + nn], identb.t[:, :]))
        g.tr(items, src_regs + identb.rg(), pregs)
        for b in range(nb):
            nn = min(128, n - b * 128)
            g.copy(g.ev_eng(), dstbuf.t[0:nn, b, c * 128:(c + 1) * 128], pab[0:nn, b * 128:(b + 1) * 128], pregs, dstbuf.rg(b))

    def bc(ap2, n):
        return ap2.unsqueeze(2).to_broadcast([128, KC, n])

    def prow(row):
        return pv.t[:, row, :]

    def headsum(src_bf, n, lhs, evac):
        for hf in range(2):
            pa, pregs = pregion(4 * n)
            items = [(pa[:, i * n:(i + 1) * n], lhs.t[:, :], src_bf.t[:, hf * 4 + i, 0:n], True, True) for i in range(4)]
            g.mm(items, src_bf.rg() + lhs.rg(), pregs)
            evac(hf * 4, pa.rearrange("p (a b) -> p a b", a=4), pregs)

    def rwkv(l, s, n, first_tile):
        gmrow = gm.t[:, (l * 2 + 0) * 3 + s, :]
        rmsnorm_mod(xT, n, gmrow, lambda c: mod.t[:, l, 0 * 8 + c, s:s + 1], out_f32=hb)
        if n > 1:
            g.op("dve", lambda e: e.tensor_tensor(out=xx.t[:, :, 1:n], in0=hb.t[:, :, 0:n - 1], in1=hb.t[:, :, 1:n], op=ALU.subtract),
                 hb.rg(), xx.rg())
        g.op("dve", lambda e: e.tensor_tensor(out=xx.t[:, :, 0], in0=carry[l].t[:, :], in1=hb.t[:, :, 0], op=ALU.subtract),
             hb.rg() + carry[l].rg(), xx.rg())
        g.op("dve", lambda e: e.tensor_copy(out=carry[l].t[:, :], in_=hb.t[:, :, n - 1]), hb.rg(), carry[l].rg())
        mixrow = V("rw_mix") + 6 * l
        mixcnt = [0]

        def mix(i):
            b = xm[mixcnt[0] % 2]
            mixcnt[0] += 1
            for c in range(KC):
                en = "dve" if c % 4 != 3 else "dve"
                g.op(en, lambda e: e.scalar_tensor_tensor(out=b.t[:, c, 0:n], in0=xx.t[:, c, 0:n], scalar=pcol(mixrow + i, c),
                                                          in1=hb.t[:, c, 0:n], op0=ALU.mult, op1=ALU.add),
                     xx.rg(c) + hb.rg(c) + pv.rg(), b.rg(c))
            return b, xs_of(b, n)

        nch = (n + 127) // 128
        csz = [min(128, n - k * 128) for k in range(nch)]
        Wb, IWb, WPb = decbufs[0], decbufs[1], decbufs[2]
        ag_b, kf_b = agbuf, kfbuf
        A3 = lambda bf_: bf_.t[:, :, 0:n]

        _, xs = mix(1)
        t1 = tmpb()

        def sink_w1(mi, mw, pa, pregs):
            g.op("act", lambda e: e.activation(out=t1.t[0:64, 0:n], in_=pa[0:64, :], func=AF.Tanh), pregs, t1.rg())
        proj(I["rw_w1"][l], xs, n, sink_w1)

        def sink_w2(mi, mw, pa, pregs):
            g.op("act", lambda e: e.activation(out=WPb.t[:, mi, 0:n], in_=pa, func=AF.Sigmoid, bias=pcol(V("rw_w0") + l, mi), scale=1.0),
                 pregs + pv.rg(), WPb.rg(mi))
            for k in range(nch):
                a_, b_ = k * 128, k * 128 + csz[k]
                g.op("dve", lambda e: e.tensor_tensor_scan(out=IWb.t[:, mi, a_:b_], data0=WPb.t[:, mi, a_:b_], data1=WPb.t[:, mi, a_:b_],
                                                           initial=0.0, op0=ALU.add, op1=ALU.bypass), WPb.rg(mi), IWb.rg(mi))
        proj(I["rw_w2"][l], [(t1.t[0:64, 0:n], t1.rg(), 64)], n, sink_w2)
        g.op("dve", lambda e: e.tensor_tensor(out=A3(WPb), in0=A3(IWb), in1=A3(WPb), op=ALU.subtract), WPb.rg() + IWb.rg(), WPb.rg())
        g.op("act", lambda e: e.activation(out=A3(WPb), in_=A3(WPb), func=AF.Exp, scale=-C0), WPb.rg(), WPb.rg())
        g.op("act", lambda e: e.activation(out=A3(Wb), in_=A3(IWb), func=AF.Exp, scale=-C0), IWb.rg(), Wb.rg())
        g.op("act", lambda e: e.activation(out=A3(IWb), in_=A3(IWb), func=AF.Exp, scale=C0), IWb.rg(), IWb.rg())
        for k in range(nch):
            g.op("pool", lambda e: e.tensor_copy(out=wc.t[:, :, k], in_=Wb.t[:, :, k * 128 + csz[k] - 1]), Wb.rg(), wc.rg())

        _, xs = mix(4)
        t2 = tmpb()

        def sink_a1(mi, mw, pa, pregs):
            g.copy("act", t2.t[0:64, 0:n], pa[0:64, :], pregs, t2.rg())
        proj(I["rw_a1"][l], xs, n, sink_a1)

        def sink_a2(mi, mw, pa, pregs):
            g.op("act", lambda e: e.activation(out=ag_b.t[:, mi, 0:n], in_=pa, func=AF.Sigmoid, bias=pcol(V("rw_a0") + l, mi), scale=1.0),
                 pregs + pv.rg(), ag_b.rg(mi))
        proj(I["rw_a2"][l], [(t2.t[0:64, 0:n], t2.rg(), 64)], n, sink_a2)

        xk, xs = mix(2)

        def sink_k(mi, mw, pa, pregs):
            g.copy(g.ev_eng(), kf_b.t[:, mi, 0:n], pa, pregs, kf_b.rg(mi))
        proj(I["rw_k"][l], xs, n, sink_k)
        g.op("dve", lambda e: e.tensor_tensor(out=A3(T1), in0=A3(kf_b), in1=bc(prow(V("rw_kk") + l), n), op=ALU.mult),
             kf_b.rg() + pv.rg(), T1.rg())
        sqk = xk
        g.op("act", lambda e: e.activation(out=A3(sqk), in_=A3(T1), func=AF.Square), T1.rg(), sqk.rg())

        def ev_kn(c0, p3, pregs):
            g.op("dve", lambda e: e.tensor_scalar(out=T2.t[:, c0:c0 + 4, 0:n], in0=p3, scalar1=1e-24, scalar2=None, op0=ALU.max),
                 pregs, T2.rg())
        headsum(sqk, n, onesb, ev_kn)
        g.op("act", lambda e: e.activation(out=A3(T2), in_=A3(T2), func=AF.Ln), T2.rg(), T2.rg())
        g.op("act", lambda e: e.activation(out=A3(T2), in_=A3(T2), func=AF.Exp, scale=-0.5), T2.rg(), T2.rg())
        g.op("dve", lambda e: e.tensor_tensor(out=A3(T1), in0=A3(T1), in1=A3(T2), op=ALU.mult), T1.rg() + T2.rg(), T1.rg())
        g.op("dve", lambda e: e.scalar_tensor_tensor(out=A3(T2), in0=A3(ag_b), scalar=-1.0, in1=bc(prow(V("rw_ka") + l), n),
                                                     op0=ALU.add, op1=ALU.mult), ag_b.rg() + pv.rg(), T2.rg())
        g.op("dve", lambda e: e.scalar_tensor_tensor(out=A3(kf_b), in0=A3(T2), scalar=1.0, in1=A3(kf_b), op0=ALU.add, op1=ALU.mult),
             T2.rg() + kf_b.rg(), kf_b.rg())
        g.op("dve", lambda e: e.scalar_tensor_tensor(out=A3(aT), in0=A3(T1), scalar=-1.0, in1=A3(WPb), op0=ALU.mult, op1=ALU.mult),
             T1.rg() + WPb.rg(), aT.rg())
        g.op("dve", lambda e: e.tensor_tensor(out=A3(T2), in0=A3(T1), in1=A3(ag_b), op=ALU.mult), T1.rg() + ag_b.rg(), T2.rg())
        g.op("dve", lambda e: e.tensor_tensor(out=A3(bT), in0=A3(T2), in1=A3(IWb), op=ALU.mult), T2.rg() + IWb.rg(), bT.rg())
        g.op("dve", lambda e: e.tensor_tensor(out=A3(kT), in0=A3(kf_b), in1=A3(IWb), op=ALU.mult), kf_b.rg() + IWb.rg(), kT.rg())
        for (src_, dst_) in ((aT, aT2), (bT, bT2)):
            g.copy("act", dst_.t[0:64, :, 0, 0:n], src_.t[0:64, :, 0:n], src_.rg(), dst_.rg())
            g.copy("act", dst_.t[64:128, :, 1, 0:n], src_.t[64:128, :, 0:n], src_.rg(), dst_.rg())
        for mi in range(KC):
            to_tm(bT.t[:, mi, 0:n], bT.rg(mi), n, btm, mi)
            to_tm(kT.t[:, mi, 0:n], kT.rg(mi), n, ktm, mi)

        xr, xs = mix(0)

        def sink_r(mi, mw, pa, pregs):
            g.copy(g.ev_eng(), T1.t[:, mi, 0:n], pa, pregs, T1.rg(mi))
        proj(I["rw_r"][l], xs, n, sink_r)
        g.op("dve", lambda e: e.tensor_tensor(out=A3(rT), in0=A3(T1), in1=A3(Wb), op=ALU.mult), T1.rg() + Wb.rg(), rT.rg())
        g.copy("act", rT2.t[0:64, :, 0, 0:n], rT.t[0:64, :, 0:n], rT.rg(), rT2.rg())
        g.copy("act", rT2.t[64:128, :, 1, 0:n], rT.t[64:128, :, 0:n], rT.rg(), rT2.rg())
        g.op("dve", lambda e: e.tensor_tensor(out=A3(T1), in0=A3(T1), in1=bc(prow(V("rw_rk") + l), n), op=ALU.mult), T1.rg() + pv.rg(), T1.rg())
        rkb = xr
        g.op("dve", lambda e: e.tensor_tensor(out=A3(rkb), in0=A3(T1), in1=A3(kf_b), op=ALU.mult), T1.rg() + kf_b.rg(), rkb.rg())

        def ev_rk(c0, p3, pregs):
            g.copy("act", bonus.t[:, c0:c0 + 4, 0:n], p3, pregs, bonus.rg())
        headsum(rkb, n, onesb, ev_rk)

        xv, xs = mix(3)
        if l > 0:
            t4 = tmpb()

            def sink_v1(mi, mw, pa, pregs):
                g.copy("act", t4.t[0:32, 0:n], pa[0:32, :], pregs, t4.rg())
            proj(I["rw_v1"][0], xs, n, sink_v1)

            def sink_v2(mi, mw, pa, pregs):
                g.op("act", lambda e: e.activation(out=T2.t[:, mi, 0:n], in_=pa, func=AF.Sigmoid, bias=pcol(V("rw_v0"), mi), scale=1.0),
                     pregs + pv.rg(), T2.rg(mi))
            proj(I["rw_v2"][0], [(t4.t[0:32, 0:n], t4.rg(), 32)], n, sink_v2)
        vdst = vfirst if l == 0 else T1

        def sink_v(mi, mw, pa, pregs):
            g.copy(g.ev_eng(), vdst.t[:, mi, 0:n], pa, pregs, vdst.rg(mi))
        proj(I["rw_v"][l], xs, n, sink_v)
        if l > 0:
            g.op("dve", lambda e: e.tensor_tensor(out=A3(kf_b), in0=A3(vfirst), in1=A3(T1), op=ALU.subtract), vfirst.rg() + T1.rg(), kf_b.rg())
            g.op("dve", lambda e: e.tensor_tensor(out=A3(kf_b), in0=A3(kf_b), in1=A3(T2), op=ALU.mult), kf_b.rg() + T2.rg(), kf_b.rg())
            g.op("dve", lambda e: e.tensor_tensor(out=A3(T1), in0=A3(T1), in1=A3(kf_b), op=ALU.add), T1.rg() + kf_b.rg(), T1.rg())
        vb = xv
        g.copy("act", A3(vb), A3(vdst), vdst.rg(), vb.rg())
        g.op("dve", lambda e: e.tensor_tensor(out=A3(bonus), in0=A3(bonus), in1=A3(vdst), op=ALU.mult), bonus.rg() + vdst.rg(), bonus.rg())
        for mi in range(KC):
            to_tm(vb.t[:, mi, 0:n], vb.rg(mi), n, vtm, mi)

        _, xs = mix(5)
        t5 = [tmpb(), tmpb()]

        def sink_g1(mi, mw, pa, pregs):
            g.op("act", lambda e: e.activation(out=t5[mi].t[0:mw, 0:n], in_=pa[0:mw, :], func=AF.Sigmoid), pregs, t5[mi].rg())
        proj(I["rw_g1"][l], xs, n, sink_g1)

        def sink_g2(mi, mw, pa, pregs):
            g.copy("act", gate.t[:, mi, 0:n], pa, pregs, gate.rg(mi))
        proj(I["rw_g2"][l], [(t5[0].t[:, 0:n], t5[0].rg(), 128), (t5[1].t[0:32, 0:n], t5[1].rg(), 32)], n, sink_g2)

        for k in range(nch):
            scan_chunk(l, k, csz[k])

        yb, ysq = xm[0], xm[1]
        g.copy("act", A3(yb), A3(yT), yT.rg(), yb.rg())

        def ev_mean(c0, p3, pregs):
            g.op("dve", lambda e: e.tensor_tensor(out=T1.t[:, c0:c0 + 4, 0:n], in0=yT.t[:, c0:c0 + 4, 0:n], in1=p3, op=ALU.subtract),
                 yT.rg() + pregs, T1.rg())
        headsum(yb, n, onesb64, ev_mean)
        g.op("act", lambda e: e.activation(out=A3(ysq), in_=A3(T1), func=AF.Square), T1.rg(), ysq.rg())

        def ev_var(c0, p3, pregs):
            g.op("act", lambda e: e.activation(out=T2.t[:, c0:c0 + 4, 0:n], in_=p3, func=AF.Ln, bias=epsb.t[:, 1:2], scale=1.0),
                 pregs + epsb.rg(), T2.rg())
        headsum(ysq, n, onesb64, ev_var)
        g.op("act", lambda e: e.activation(out=A3(T2), in_=A3(T2), func=AF.Exp, scale=-0.5), T2.rg(), T2.rg())
        g.op("dve", lambda e: e.tensor_tensor(out=A3(T1), in0=A3(T1), in1=A3(T2), op=ALU.mult), T1.rg() + T2.rg(), T1.rg())
        g.op("dve", lambda e: e.tensor_tensor(out=A3(T1), in0=A3(T1), in1=bc(prow(V("rw_lnw") + l), n), op=ALU.mult), T1.rg() + pv.rg(), T1.rg())
        g.op("dve", lambda e: e.tensor_tensor(out=A3(T1), in0=A3(T1), in1=bc(prow(V("rw_lnb") + l), n), op=ALU.add), T1.rg() + pv.rg(), T1.rg())
        g.op("dve", lambda e: e.tensor_tensor(out=A3(T1), in0=A3(T1), in1=A3(bonus), op=ALU.add), T1.rg() + bonus.rg(), T1.rg())
        g.op("dve", lambda e: e.tensor_tensor(out=A3(hbf), in0=A3(T1), in1=A3(gate), op=ALU.mult), T1.rg() + gate.rg(), hbf.rg())
        proj(I["rw_o"][l], xs_of(hbf, n), n, resid_sink(l, s, n))

    def scan_chunk(l, k, nn):
        co = k * 128
        nlev = 0
        while (1 << (nlev + 1)) < nn:
            nlev += 1
        S_f, S_b = Sf[l], Sb[l]
        for grp in range(4):
            heads = [grp * 4 + i for i in range(4)]
            def hv(buf, h):
                pb_ = (h % 2) * 64
                return buf.t[pb_:pb_ + 64, h // 2, co:co + nn]
            specs = [(0, aT, bT2, m_sl, XR[0]), (1, bT, aT2, m_su, XTb[0]), (2, kT, aT2, m_su, AKTb),
                     (3, bT, rT2, m_iu, ARBb), (4, kT, rT2, m_iu, ARKb)]
            for (bk, L, Rr, msk, dst) in specs:
                items = []
                rd = []
                for pi in range(2):
                    hp = grp * 2 + pi
                    o = PS[bk].t[0:nn, pi * 256:(pi + 1) * 256].rearrange("p (a b) -> p a b", a=2)[:, :, 0:nn]
                    items.append((o, L.t[:, hp, co:co + nn], Rr.t[:, hp, :, co:co + nn], True, True))
                    rd += L.rg(hp) + Rr.rg(hp)
                g.mm(items, rd, PS[bk].rg())
                pin = PS[bk].t[0:nn, :].rearrange("p (a b) -> p a b", a=4)[:, :, 0:nn]
                dsto = dst.t[0:nn, :, 0:nn] if bk < 2 else dst.t[0:nn, :, 0:nn]
                g.op("dve", lambda e: e.tensor_tensor(out=dsto, in0=pin,
                                                      in1=msk.t[0:nn, 0:nn].unsqueeze(1).to_broadcast([nn, 4, nn]), op=ALU.mult),
                     PS[bk].rg() + msk.rg(), dst.rg())
            items = []
            rd = list(AKTb.rg()) + S_b.rg() + vtm.rg(k)
            for pi in range(2):
                hp = grp * 2 + pi
                o2 = PS[5].t[0:nn, pi * 128:(pi + 1) * 128]
                items.append((o2, aT.t[:, hp, co:co + nn], S_b.t[:, hp, :], True, False))
                rd += aT.rg(hp)
                for half in range(2):
                    h = hp * 2 + half
                    i = pi * 2 + half
                    o = PS[5].t[0:nn, i * 64:(i + 1) * 64]
                    items.append((o, AKTb.t[0:nn, i, 0:nn], vtm.t[0:nn, k, h * 64:(h + 1) * 64], False, half == 1))
            g.mm(items, rd, PS[5].rg(0, 2))
            pr = PS[5].t[0:nn, 0:256].rearrange("p (a b) -> p a b", a=4)
            g.op("act", lambda e: e.activation(out=XR[0].t[0:nn, :, 128:192], in_=pr, func=AF.Copy), PS[5].rg(0, 2), XR[0].rg())
            cur = 0
            BK = {0: (PS[5], PS[1]), 1: (PS[6], PS[7])}
            for lev in range(nlev + 1):
                XRc, XTc = XR[cur], XTb[cur]
                nxt = 1 - cur
                needX = lev < nlev - 1
                for pi in range(2):
                    bxr, bxt = BK[pi]
                    hs = (2 * pi, 2 * pi + 1)
                    if needX:
                        items = [(bxr.t[0:nn, j_ * 192:(j_ + 1) * 192], XTc.t[0:nn, i, 0:nn], XRc.t[0:nn, i, :], True, True)
                                 for j_, i in enumerate(hs)]
                    else:
                        items = [(bxr.t[0:nn, j_ * 192 + 128:(j_ + 1) * 192], XTc.t[0:nn, i, 0:nn], XRc.t[0:nn, i, 128:192], True, True)
                                 for j_, i in enumerate(hs)]
                    g.mm(items, XTc.rg(pi) + XRc.rg(pi), bxr.rg())
                    if lev < nlev:
                        items = [(bxt.t[0:nn, j_ * 128:j_ * 128 + nn], XRc.t[0:nn, i, 0:nn], XTc.t[0:nn, i, 0:nn], True, True)
                                 for j_, i in enumerate(hs)]
                        g.mm(items, XRc.rg(pi) + XTc.rg(pi), bxt.rg())
                for pi in range(2):
                    bxr, bxt = BK[pi]
                    h0 = 2 * pi
                    pv3 = bxr.t[0:nn, 0:384].rearrange("p (a b) -> p a b", a=2)
                    g.op("dve", lambda e: e.tensor_tensor(out=XR[nxt].t[0:nn, h0:h0 + 2, 128:192], in0=XRc.t[0:nn, h0:h0 + 2, 128:192],
                                                          in1=pv3[:, :, 128:192], op=ALU.add), XRc.rg(pi) + bxr.rg(), XR[nxt].rg(pi))
                    if needX:
                        g.copy("dve", XR[nxt].t[0:nn, h0:h0 + 2, 0:nn], pv3[:, :, 0:nn], bxr.rg(), XR[nxt].rg(pi))
                    if lev < nlev:
                        g.copy("act", XTb[nxt].t[0:nn, h0:h0 + 2, 0:nn],
                               bxt.t[0:nn, 0:256].rearrange("p (a b) -> p a b", a=2)[:, :, 0:nn], bxt.rg(), XTb[nxt].rg(pi))
                cur = nxt
            Rfin = XR[cur]
            g.op("act", lambda e: e.activation(out=Ub.t[0:nn, grp * 4:grp * 4 + 4, :], in_=Rfin.t[0:nn, :, 128:192], func=AF.Copy), Rfin.rg(), Ub.rg(grp))
            for pi in range(2):
                hp = grp * 2 + pi
                for half in range(2):
                    h = hp * 2 + half
                    i = pi * 2 + half
                    yreg = PS[2 + half]
                    yo = yreg.t[:, pi * 128:pi * 128 + nn]
                    items = [(yo, S_b.t[:, hp, :], rT.t[:, hp, co:co + nn], True, False),
                             (yo, Ub.t[0:nn, hp * 2:hp * 2 + 2, :].rearrange("p a b -> p (a b)"), ARBb.t[0:nn, i, 0:nn], False, False),
                             (yo, vtm.t[0:nn, k, hp * 128:(hp + 1) * 128], ARKb.t[0:nn, i, 0:nn], False, True)]
                    g.mm(items, S_b.rg() + rT.rg(hp) + Ub.rg(grp) + ARBb.rg() + ARKb.rg() + vtm.rg(k), yreg.rg(pi, pi + 1))
                    so = yreg.t[:, 256 + pi * 64:256 + (pi + 1) * 64]
                    items = [(so, btm.t[0:nn, k, hp * 128:(hp + 1) * 128], Ub.t[0:nn, h, :], True, False),
                             (so, ktm.t[0:nn, k, hp * 128:(hp + 1) * 128], vtm.t[0:nn, k, h * 64:(h + 1) * 64], False, True)]
                    g.mm(items, btm.rg(k) + ktm.rg(k) + Ub.rg(grp) + vtm.rg(k), yreg.rg(2, 3))
            for half in range(2):
                pb_ = half * 64
                yreg = PS[2 + half]
                g.copy(g.ev_eng(), yT.t[pb_:pb_ + 64, grp * 2:grp * 2 + 2, co:co + nn],
                       yreg.t[pb_:pb_ + 64, 0:256].rearrange("p (a b) -> p a b", a=2)[:, :, 0:nn], yreg.rg(0, 2), yT.rg(grp * 2, grp * 2 + 2))
                sps = yreg.t[pb_:pb_ + 64, 256:384].rearrange("p (a b) -> p a b", a=2)
                sfv = S_f.t[pb_:pb_ + 64, grp * 2:grp * 2 + 2, :]
                g.op("dve", lambda e: e.tensor_tensor(out=sfv, in0=sfv, in1=sps, op=ALU.add), S_f.rg() + yreg.rg(2, 3), S_f.rg())
                g.op("dve", lambda e: e.tensor_tensor(out=sfv, in0=sfv, in1=wc.t[pb_:pb_ + 64, grp * 2:grp * 2 + 2, k:k + 1].to_broadcast([64, 2, 64]),
                                                      op=ALU.mult), S_f.rg() + wc.rg(), S_f.rg())
                g.op("pool", lambda e: e.tensor_copy(out=S_b.t[pb_:pb_ + 64, grp * 2:grp * 2 + 2, pb_:pb_ + 64], in_=sfv), S_f.rg(), S_b.rg())

    def headnorm_all(n, gain_row, scale, out_ap3, out_regs):
        sq = xm[0]
        g.op("act", lambda e: e.activation(out=sq.t[:, :, 0:n], in_=T1.t[:, :, 0:n], func=AF.Square), T1.rg(), sq.rg())

        def ev(c0, p3, pregs):
            g.op("act", lambda e: e.activation(out=T2.t[:, c0:c0 + 4, 0:n], in_=p3, func=AF.Ln, bias=epsb.t[:, 0:1], scale=1.0),
                 pregs + epsb.rg(), T2.rg())
        headsum(sq, n, onesb64, ev)
        g.op("act", lambda e: e.activation(out=T2.t[:, :, 0:n], in_=T2.t[:, :, 0:n], func=AF.Exp, scale=-0.5), T2.rg(), T2.rg())
        g.op("dve", lambda e: e.tensor_scalar(out=T1.t[:, :, 0:n], in0=T1.t[:, :, 0:n], scalar1=pcol(gain_row, 0), scalar2=float(scale),
                                              op0=ALU.mult, op1=ALU.mult), T1.rg() + pv.rg(), T1.rg())
        g.op("dve", lambda e: e.tensor_tensor(out=out_ap3, in0=T1.t[:, :, 0:n], in1=T2.t[:, :, 0:n], op=ALU.mult), T1.rg() + T2.rg(), out_regs)

    def raw_sink(n):
        def sink(mi, mw, pa, pregs):
            g.copy(g.ev_eng(), T1.t[:, mi, 0:n], pa, pregs, T1.rg(mi))
        return sink

    def shared_kv(sq_, s, t0, n, blk0):
        rmsnorm_mod(xT, n, gkv.t[:, s, :], lambda c: kvm.t[:, c, s:s + 1], out_bf=hbf)
        xs = xs_of(hbf, n)
        kind, bidx, T = sq_
        nb = (n + 127) // 128
        want_out = (kind == "s") or (t0 + n > T - KEEP)

        proj(I["w_kv"][:, 0:D], xs, n, raw_sink(n))
        headnorm_all(n, V("k_norm"), 1.0, yT.t[:, :, 0:n], yT.rg())
        for b in range(nb):
            nn = min(128, n - b * 128)
            slot = (blk0 + b) % NSLOT
            g.copy("act", Kring.t[:, :, slot * 128:slot * 128 + nn], yT.t[:, :, b * 128:b * 128 + nn], yT.rg(), Kring.rg(slot))
        for half in range(2):
            wsl = []
            for q4 in range(2):
                m0 = D + half * 512 + q4 * 256
                wsl.append(wslab(I["w_kv"][:, m0:m0 + 256].rearrange("(kc p) m -> p kc m", p=128)))
            for b in range(nb):
                nn = min(128, n - b * 128)
                slot = (blk0 + b) % NSLOT
                pb = PS[6 + (b + half) % 2]
                for q4 in range(2):
                    wb, wregs = wsl[q4]
                    g.mm([(pb.t[0:nn, q4 * 256:(q4 + 1) * 256], hbf.t[:, c, b * 128:b * 128 + nn], wb[:, c, :], c == 0, c == KC - 1)
                          for c in range(KC)], hbf.rg() + wregs, pb.rg(2 * q4, 2 * q4 + 2))
                g.copy("act", Vring.t[0:nn, slot, half * 512:(half + 1) * 512], pb.t[0:nn, :], pb.rg(), Vring.rg(slot))
                if want_out:
                    g.copy("dve", kvst.t[0:nn, b, half * 512:(half + 1) * 512], pb.t[0:nn, :], pb.rg(), kvst.rg(b))
        if want_out:
            if kind == "s":
                store_tok(yT, n, O["k_s"][0])
                g.dma(kvst, O["v_s"][0][0:n, :], kvst.t[0:n, 0, :], kvst.rg(0), [], is_out=True)
            else:
                r0 = t0 - (T - KEEP)
                store_tok(yT, n, O["k_p"][bidx][r0:r0 + n, :])
                for b in range(nb):
                    g.dma(kvst, O["v_p"][bidx][r0 + b * 128:r0 + (b + 1) * 128, :], kvst.t[:, b, :], kvst.rg(b), [], is_out=True)

    def attn(l, s, n, blk0, sq_):
        j = l - 2
        kind, bidx, T = sq_
        gmrow = gm.t[:, (l * 2 + 0) * 3 + s, :]
        rmsnorm_mod(xT, n, gmrow, lambda c: mod.t[:, l, 0 * 8 + c, s:s + 1], out_bf=hbf)

        proj(I["wb_q"][j], xs_of(hbf, n), n, raw_sink(n))
        headnorm_all(n, V("q_norm") + j, 0.125, qT.t[:, :, 0:n], qT.rg())
        nu = (n + 127) // 128
        units = [(h, u) for h in range(16) for u in range(nu)]
        info = {}

        def stage1(h, u):
            bb = biasb[h % 2]
            if u == 0:
                g.dma(bb, bb.t[:], BIASB.t[j * 16 + h].rearrange("p (t q) -> p t q", t=5), BIASB.rg(j * 16 + h), bb.rg())
            hp, half = h // 2, h % 2
            pb_ = half * 64
            nq = min(128, n - u * 128)
            qb = blk0 + u
            tiles = []
            for t in range(5):
                kb = qb - 4 + t
                if kb < 0:
                    continue
                tiles.append((t, kb % NSLOT, 128 if t < 4 else nq))
            pt = PT[(h * nu + u) % 2]
            SA, SB = (PS[0], PS[1]) if h % 2 == 0 else (PS[4], PS[5])

            def mmt(t, slot, nk, o):
                return [(o, Kring.t[pb_:pb_ + 64, hp, slot * 128:slot * 128 + nk], qT.t[pb_:pb_ + 64, hp, u * 128:u * 128 + nq], True, False),
                        (o, identb.t[0:nk, 0:nk], bb.t[0:nk, 4 - t, 0:nq], False, True)]
            full = [x for x in tiles if x[0] < 4]
            if full:
                items, rd = [], []
                for (t, slot, nk) in full:
                    items += mmt(t, slot, nk, SA.t[0:nk, t * 128:t * 128 + nq])
                    rd += Kring.rg(slot)
                g.mm(items, rd + qT.rg(hp) + bb.rg() + identb.rg(), SA.rg())
                t0_ = full[0][0]
                src = SA.t[:, t0_ * 128:512].rearrange("p (a b) -> p a b", a=4 - t0_)[:, :, 0:nq]
                g.op("act", lambda e: e.activation(out=pt.t[:, t0_:4, 0:nq], in_=src, func=AF.Exp), SA.rg(), pt.rg())
            (t, slot, nk) = tiles[-1]
            o = SB.t[0:nk, 0:nq]
            g.mm(mmt(t, slot, nk, o), Kring.rg(slot) + qT.rg(hp) + bb.rg() + identb.rg(), SB.rg())
            g.op("act", lambda e: e.activation(out=pt.t[0:nk, 4, 0:nq], in_=o, func=AF.Exp), SB.rg(), pt.rg())
            info[(h, u)] = (tiles, pt, nq)

        def stage2(h, u):
            tiles, pt, nq = info[(h, u)]
            hp, half = h // 2, h % 2
            pb_ = half * 64
            ob = PS[2 + half]
            oo = ob.t[:, u * 128:u * 128 + nq]
            dd = ob.t[:, 256 + u * 128:256 + u * 128 + nq]
            items = []
            for ti, (t, slot, nk) in enumerate(tiles):
                items.append((oo, Vring.t[0:nk, slot, hp * 128:(hp + 1) * 128], pt.t[0:nk, t, 0:nq], ti == 0, ti == len(tiles) - 1))
            for ti, (t, slot, nk) in enumerate(tiles):
                items.append((dd, ones1.t[0:nk, :], pt.t[0:nk, t, 0:nq], ti == 0, ti == len(tiles) - 1))
            vr = []
            for (t, slot, nk) in tiles:
                vr += Vring.rg(slot)
            g.mm(items, vr + pt.rg() + ones1.rg(), ob.rg())
            rc = tmp()
            g.op("dve", lambda e: e.reciprocal(out=rc.t[pb_:pb_ + 64, 0:nq], in_=dd[pb_:pb_ + 64, :]), ob.rg(), rc.rg())
            g.op("dve", lambda e: e.tensor_tensor(out=oT.t[pb_:pb_ + 64, hp, u * 128:u * 128 + nq], in0=oo[pb_:pb_ + 64, :],
                                                  in1=rc.t[pb_:pb_ + 64, 0:nq], op=ALU.mult), ob.rg() + rc.rg(), oT.rg(hp))

        for i, (h, u) in enumerate(units):
            stage1(h, u)
            if i > 0:
                stage2(*units[i - 1])
        stage2(*units[-1])
        proj(I["wb_o"][j], xs_of(oT, n), n, resid_sink(l, s, n))

    def state_load(l):
        g.dma(sst, sst.t[:], I["state_wkv"][l, 0].rearrange("h i j -> i h j"), [], sst.rg())
        for half in range(2):
            pb = PS[6 + half]
            items = []
            for hp in range(KC):
                h = hp * 2 + half
                items.append((pb.t[0:64, hp * 64:(hp + 1) * 64], sst.t[0:64, h, :], ident.t[0:64, 0:64]))
            g.tr(items, sst.rg() + ident.rg(), pb.rg())
            g.op("dve", lambda e: e.tensor_copy(out=Sf[l].t[half * 64:half * 64 + 64, :, :],
                                                in_=pb.t[0:64, :].rearrange("p (a b) -> p a b", a=KC)), pb.rg(), Sf[l].rg())
            g.op("dve", lambda e: e.tensor_copy(out=Sb[l].t[half * 64:half * 64 + 64, :, half * 64:half * 64 + 64],
                                                in_=pb.t[0:64, :].rearrange("p (a b) -> p a b", a=KC)), pb.rg(), Sb[l].rg())

    def state_store(l, dst):
        for half in range(2):
            pb = PS[6 + half]
            items = []
            for hp in range(KC):
                items.append((pb.t[0:64, hp * 64:(hp + 1) * 64], Sf[l].t[half * 64:half * 64 + 64, hp, :], ident.t[half * 64:half * 64 + 64, half * 64:half * 64 + 64]))
            g.tr(items, Sf[l].rg() + ident.rg(), pb.rg())
            g.op("dve", lambda e: e.tensor_copy(out=sst.t[:, :, :].rearrange("p (a two) b -> p a two b", two=2)[:, :, half, :],
                                                in_=pb.t[0:64, :].rearrange("p (a b) -> p a b", a=KC)), pb.rg(), sst.rg())
        g.dma(sst, dst.rearrange("h i j -> i h j"), sst.t[:], sst.rg(), [], is_out=True)

    def shift_store(l, dst_row):
        pb = PS[6]
        g.tr([(pb.t[0:KC, 0:128], carry[l].t[:, :], ident.t[:, :])], carry[l].rg() + ident.rg(), pb.rg(0, 1))
        st_ = tmp()
        g.copy("dve", st_.t[0:KC, 0:128], pb.t[0:KC, 0:128], pb.rg(0, 1), st_.rg())
        g.dma(st_, dst_row.rearrange("(c p) -> c p", p=128), st_.t[0:KC, 0:128], st_.rg(), [], is_out=True)

    seqs = [("p", 0, TP), ("p", 1, TP), ("s", 0, TS)]
    for s, sq_ in enumerate(seqs):
        kind, bidx, T = sq_
        for l in range(2):
            if kind == "p":
                g.op("dve", lambda e: e.memset(carry[l].t[:], 0.0), [], carry[l].rg())
                g.op("dve", lambda e: e.memset(Sf[l].t[:], 0.0), [], Sf[l].rg())
                g.op("dve", lambda e: e.memset(Sb[l].t[:], 0.0), [], Sb[l].rg())
            else:
                r = V("state_shift") + l
                g.op("dve", lambda e: e.tensor_copy(out=carry[l].t[:, :], in_=pv.t[:, r, :]), pv.rg(), carry[l].rg())
                state_load(l)
        if kind == "s":
            for slot in range(4):
                load_xT(I["cache_k"][0][slot * 128:(slot + 1) * 128, :], 128, yT)
                for c in range(KC):
                    g.op("pool", lambda e: e.tensor_copy(out=Kring.t[:, c, slot * 128:(slot + 1) * 128], in_=yT.t[:, c, 0:128]),
                         yT.rg(c), Kring.rg(slot))
                g.dma(kvst, kvst.t[:, 0, :], I["cache_v"][0][slot * 128:(slot + 1) * 128, :], [], kvst.rg(0))
                g.op("pool", lambda e: e.tensor_copy(out=Vring.t[:, slot, :], in_=kvst.t[:, 0, :]), kvst.rg(0), Vring.rg(slot))
        ntile = (T + NT - 1) // NT
        for ti in range(ntile):
            t0 = ti * NT
            n = min(NT, T - t0)
            blk0 = (t0 // 128) if kind == "p" else 4
            ck(31)
            xsrc = I["x_prompt"][bidx] if kind == "p" else I["x_sample"][0]
            load_xT(xsrc[t0:t0 + n, :], n, xT)
            ck(4)
            for l in range(2):
                rwkv(l, s, n, ti == 0)
                ck(5 + 2 * l)
                ffn(l, s, n)
                ck(6 + 2 * l)
            shared_kv(sq_, s, t0, n, blk0)
            ck(9)
            for l in range(2, 4):
                attn(l, s, n, blk0, sq_)
                ck(10 + 2 * (l - 2))
                ffn(l, s, n)
                ck(11 + 2 * (l - 2))
            ydst = O["y_prompt"][bidx] if kind == "p" else O["y_sample"][0]
            store_tok(xT, n, ydst[t0:t0 + n, :])
        for l in range(2):
            if kind == "p":
                shift_store(l, O["shift_p"][l, bidx])
                state_store(l, O["wkv_p"][l, bidx])
            else:
                shift_store(l, O["shift_s"][l, 0])
                state_store(l, O["wkv_s"][l, 0])


_CACHE = {}


def _run(inputs, TP, TS, ncores):
    key = (TP, TS)
    if key not in _CACHE:
        _CACHE[key] = build(TP, TS)
    nc, st = _CACHE[key]
    f = lambda a: np.ascontiguousarray(np.asarray(a, dtype=np.float32))
    in_maps = []
    for c in range(ncores):
        m = {}
        for k, v in inputs.items():
            v = f(v)
            if k == "x_prompt":
                m[k] = f(v[2 * c:2 * c + 2])
            elif k == "c_prompt":
                m[k] = f(v[2 * c:2 * c + 2])
            elif k in ("x_sample", "c_sample"):
                m[k] = f(v[c:c + 1])
            elif k in ("state_shift", "state_wkv"):
                m[k] = f(v[:, c:c + 1])
            elif k in ("cache_k", "cache_v"):
                m[k] = f(v[c:c + 1].reshape(1, 512, D))
            elif k == "rw_rk":
                m[k] = f(v.reshape(2, D))
            else:
                m[k] = v
        in_maps.append(m)
    res = run_bass_kernel_spmd(nc, in_maps, core_ids=list(range(ncores)))
    R = res.results
    cat = lambda k, ax: np.concatenate([R[c][k] for c in range(ncores)], axis=ax)
    KEEP = min(512, TP)
    y_prompt = cat("y_prompt", 0)
    y_sample = cat("y_sample", 0)
    shift_p = cat("shift_p", 1)
    wkv_p = cat("wkv_p", 1)
    k_p = cat("k_p", 0).reshape(2 * ncores, KEEP, 16, 64)
    v_p = cat("v_p", 0).reshape(2 * ncores, KEEP, 16, 64)
    shift_s = cat("shift_s", 1)
    wkv_s = cat("wkv_s", 1)
    k_s = cat("k_s", 0).reshape(ncores, TS, 16, 64)
    v_s = cat("v_s", 0).reshape(ncores, TS, 16, 64)
    return (y_prompt, y_sample, shift_p, wkv_p, k_p, v_p, shift_s, wkv_s, k_s, v_s)


def kernel(**inputs):
    TP = inputs["x_prompt"].shape[1]
    TS = inputs["x_sample"].shape[1]
    return _run(inputs, TP, TS, 8)
```

```python
import contextlib
import numpy as np
import concourse.bass as bass
import concourse.mybir as mybir
from concourse.bass_utils import run_bass_kernel_spmd

F32 = mybir.dt.float32
BF16 = mybir.dt.bfloat16
F32R = mybir.dt.float32r
AF = mybir.ActivationFunctionType
ALU = mybir.AluOpType

D = 1024
KC = 8
FF = 2816
FC = 22
NT = 128
NBLK = NT // 128
NL = 4
NEG = -30000.0
C0 = float(np.exp(-0.5))


import os
STOP = int(os.environ.get("K_STOP", "0"))


class _Stop(Exception):
    pass


def ck(k):
    if STOP == k:
        raise _Stop()


class Reg:
    __slots__ = ("w", "rd", "excl")

    def __init__(self):
        self.w = None
        self.rd = {}
        self.excl = False


class Buf:
    def __init__(self, t, nreg):
        self.t = t
        self.regs = [Reg() for _ in range(nreg)]
        self.dkey = None
        self.dcnt = 0

    def rg(self, lo=None, hi=None):
        if lo is None or len(self.regs) == 1:
            return self.regs
        if hi is None:
            return [self.regs[lo]]
        return self.regs[lo:hi]


class Gen:
    def __init__(self, nc, stack, TP, TS):
        self.nc = nc
        self.st = stack
        self.TP = TP
        self.TS = TS
        self.eng = {"pe": nc.tensor, "act": nc.scalar, "dve": nc.vector, "pool": nc.gpsimd, "sp": nc.sync}
        self.semobj = {}
        self.cnt = {}
        self.seen = {n: {} for n in self.eng}
        for n in self.eng:
            self.semobj[n] = stack.enter_context(nc.semaphore("s_" + n))
            self.cnt[n] = 0
        self.ndsem = 0
        self.out_events = {}
        self.evrr = 0

    def sb(self, name, shape, dt, nreg=1):
        t = self.st.enter_context(self.nc.sbuf_tensor(name, list(shape), dt))
        return Buf(t, nreg)

    def ps(self, name, nreg=1):
        t = self.st.enter_context(self.nc.psum_tensor(name, [128, 512], F32))
        b = Buf(t, 1)
        b.regs[0].excl = True
        return b

    def dram(self, name, shape, dt, nreg=1):
        t = self.nc.dram_tensor(name, list(shape), dt, kind="Internal").ap()
        return Buf(t, nreg)

    def _wait(self, en, ev):
        key, val = ev
        if self.seen[en].get(key, 0) >= val:
            return
        self.eng[en].wait_ge(self.semobj[key], val)
        self.seen[en][key] = val

    def _deps(self, en, rd, wr):
        ex = [r for r in rd if r.excl]
        if ex:
            rd = [r for r in rd if not r.excl]
            wr = list(wr) + ex
        evs = []
        for r in rd:
            if r.w is not None:
                evs.append(r.w)
        for r in wr:
            if r.w is not None:
                evs.append(r.w)
            for k, ev in r.rd.items():
                evs.append(ev)
        for ev in evs:
            if en == "pe" and ev[0] == "pe":
                continue
            self._wait(en, ev)

    def _mark(self, ev, rd, wr):
        ex = [r for r in rd if r.excl]
        if ex:
            rd = [r for r in rd if not r.excl]
            wr = list(wr) + ex
        for r in rd:
            r.rd[ev[0]] = ev
        for r in wr:
            r.w = ev
            r.rd = {}

    def op(self, en, fn, rd, wr):
        self._deps(en, rd, wr)
        ins = fn(self.eng[en])
        self.cnt[en] += 1
        ins.then_inc(self.semobj[en], 1)
        self._mark((en, self.cnt[en]), rd, wr)

    def mm(self, items, rd, wr):
        self._deps("pe", rd, wr)
        ins = None
        for (o, l, r, s, e) in items:
            ins = self.nc.tensor.matmul(o, lhsT=l, rhs=r, start=s, stop=e)
        self.cnt["pe"] += 1
        ins.then_inc(self.semobj["pe"], 1)
        self._mark(("pe", self.cnt["pe"]), rd, wr)

    def tr(self, items, rd, wr):
        self._deps("pe", rd, wr)
        ins = None
        for (o, i, idn) in items:
            ins = self.nc.tensor.transpose(out=o, in_=i, identity=idn)
        self.cnt["pe"] += 1
        ins.then_inc(self.semobj["pe"], 1)
        self._mark(("pe", self.cnt["pe"]), rd, wr)

    def dma(self, buf, out, in_, rd, wr, is_out=False, slow=False):
        en = "sp"
        if buf.dkey is None:
            buf.dkey = "d%d" % self.ndsem
            self.ndsem += 1
            self.semobj[buf.dkey] = self.st.enter_context(self.nc.semaphore(buf.dkey))
        self._deps(en, rd, wr)
        if buf.dcnt > 0:
            self._wait(en, (buf.dkey, buf.dcnt))
        if slow:
            ins = self.nc.sync.dma_start(out=out, in_=in_, allow_slow_non_contiguous=True)
        else:
            ins = self.nc.sync.dma_start(out=out, in_=in_)
        buf.dcnt += 16
        ins.then_inc(self.semobj[buf.dkey], 16)
        ev = (buf.dkey, buf.dcnt)
        self._mark(ev, rd, wr)
        if is_out:
            self.out_events[buf.dkey] = buf.dcnt

    def finish(self):
        for k, v in self.out_events.items():
            self._wait("sp", (k, v))

    def ev_eng(self):
        self.evrr ^= 1
        return "act" if self.evrr else "dve"

    def copy(self, en, out, in_, rd, wr):
        if en == "act":
            self.op("act", lambda e: e.activation(out=out, in_=in_, func=AF.Copy), rd, wr)
        else:
            self.op(en, lambda e: e.tensor_copy(out=out, in_=in_), rd, wr)


def build(TP, TS):
    nc = bass.Bass("TRN2", target_bir_lowering=False)
    st = contextlib.ExitStack()
    g = Gen(nc, st, TP, TS)
    try:
        _build(nc, st, g, TP, TS)
    except _Stop:
        pass
    g.finish()
    return nc, st


def _build(nc, st, g, TP, TS):
    NBP = 2

    def din(name, shape):
        return nc.dram_tensor(name, list(shape), F32, kind="ExternalInput").ap()

    def dout(name, shape):
        return nc.dram_tensor(name, list(shape), F32, kind="ExternalOutput").ap()

    KEEP = min(512, TP)
    I = dict(
        x_prompt=din("x_prompt", [NBP, TP, D]), x_sample=din("x_sample", [1, TS, D]),
        c_prompt=din("c_prompt", [NBP, D]), c_sample=din("c_sample", [1, D]),
        state_shift=din("state_shift", [2, 1, D]), state_wkv=din("state_wkv", [2, 1, 16, 64, 64]),
        cache_k=din("cache_k", [1, 512, D]), cache_v=din("cache_v", [1, 512, D]),
        w_ada=din("w_ada", [4, D, 6 * D]), b_ada=din("b_ada", [4, 6 * D]),
        norm_mix=din("norm_mix", [4, D]), norm_ffn=din("norm_ffn", [4, D]),
        rw_mix=din("rw_mix", [2, 6, D]), rw_r=din("rw_r", [2, D, D]), rw_k=din("rw_k", [2, D, D]),
        rw_v=din("rw_v", [2, D, D]), rw_o=din("rw_o", [2, D, D]), rw_w0=din("rw_w0", [2, D]),
        rw_w1=din("rw_w1", [2, D, 64]), rw_w2=din("rw_w2", [2, 64, D]), rw_a0=din("rw_a0", [2, D]),
        rw_a1=din("rw_a1", [2, D, 64]), rw_a2=din("rw_a2", [2, 64, D]), rw_v0=din("rw_v0", [1, D]),
        rw_v1=din("rw_v1", [1, D, 32]), rw_v2=din("rw_v2", [1, 32, D]), rw_g1=din("rw_g1", [2, D, 160]),
        rw_g2=din("rw_g2", [2, 160, D]), rw_kk=din("rw_kk", [2, D]), rw_ka=din("rw_ka", [2, D]),
        rw_rk=din("rw_rk", [2, D]), rw_lnw=din("rw_lnw", [2, D]), rw_lnb=din("rw_lnb", [2, D]),
        kv_ada_w=din("kv_ada_w", [D, 2 * D]), kv_ada_b=din("kv_ada_b", [2 * D]), kv_norm=din("kv_norm", [D]),
        w_kv=din("w_kv", [D, 2 * D]), k_norm=din("k_norm", [64]), wb_q=din("wb_q", [2, D, D]),
        q_norm=din("q_norm", [2, 64]), rel_bias=din("rel_bias", [2, 16, 257]), wb_o=din("wb_o", [2, D, D]),
        w_gate=din("w_gate", [4, D, FF]), w_up=din("w_up", [4, D, FF]), w_down=din("w_down", [4, FF, D]),
    )
    O = dict(
        y_prompt=dout("y_prompt", [NBP, TP, D]), y_sample=dout("y_sample", [1, TS, D]),
        shift_p=dout("shift_p", [2, NBP, D]), wkv_p=dout("wkv_p", [2, NBP, 16, 64, 64]),
        k_p=dout("k_p", [NBP, KEEP, D]), v_p=dout("v_p", [NBP, KEEP, D]),
        shift_s=dout("shift_s", [2, 1, D]), wkv_s=dout("wkv_s", [2, 1, 16, 64, 64]),
        k_s=dout("k_s", [1, TS, D]), v_s=dout("v_s", [1, TS, D]),
    )

    ident = g.sb("ident", [128, 128], F32)
    identb = g.sb("identb", [128, 128], BF16)
    onesm = g.sb("onesm", [128, 128], BF16)
    onesb = g.sb("onesb", [128, 128], BF16)
    onesb64 = g.sb("onesb64", [128, 128], BF16)
    ones1 = g.sb("ones1", [128, 128], BF16)
    m_sl = g.sb("m_sl", [128, 128], F32)
    m_su = g.sb("m_su", [128, 128], F32)
    m_iu = g.sb("m_iu", [128, 128], F32)
    epsb = g.sb("epsb", [128, 4], F32)
    NV = 80
    rows = g.sb("rows", [NV, D], F32)
    pv = g.sb("pv", [128, NV, 8], F32)
    scT = g.sb("scT", [128, 8, 4], F32)
    mod = g.sb("mod", [128, NL, 48, 3], F32)
    kvm = g.sb("kvm", [128, 16, 3], F32)
    gm = g.sb("gm", [128, NL * 2 * 3, 8], F32)
    gkv = g.sb("gkv", [128, 3, 8], F32)

    PS = [g.ps("ps%d" % i) for i in range(8)]

    xin = g.sb("xin", [128, NBLK, D], F32, NBLK)
    xT = g.sb("xT", [128, KC, NT], F32, KC)
    hb = g.sb("hb", [128, KC, NT], F32, KC)
    xx = g.sb("xx", [128, KC, NT], F32, KC)
    hbf = g.sb("hbf", [128, KC, NT], BF16, KC)
    xm = [g.sb("xm%d" % i, [128, KC, NT], BF16, KC) for i in range(2)]
    rT = g.sb("rT", [128, KC, NT], BF16, KC)
    kT = g.sb("kT", [128, KC, NT], BF16, KC)
    bT = g.sb("bT", [128, KC, NT], BF16, KC)
    aT = g.sb("aT", [128, KC, NT], BF16, KC)
    aT2 = g.sb("aT2", [128, KC, 2, NT], BF16, KC)
    bT2 = g.sb("bT2", [128, KC, 2, NT], BF16, KC)
    rT2 = g.sb("rT2", [128, KC, 2, NT], BF16, KC)
    vtm = g.sb("vtm", [128, NBLK, D], BF16, NBLK)
    ktm = g.sb("ktm", [128, NBLK, D], BF16, NBLK)
    btm = g.sb("btm", [128, NBLK, D], BF16, NBLK)
    vfirst = g.sb("vfirst", [128, KC, NT], F32, KC)
    T1 = g.sb("T1", [128, KC, NT], F32, KC)
    T2 = g.sb("T2", [128, KC, NT], F32, KC)
    decbufs = [g.sb("decb%d" % i, [128, KC, NT], F32, KC) for i in range(3)]
    agbuf = g.sb("agbuf", [128, KC, NT], F32, KC)
    kfbuf = g.sb("kfbuf", [128, KC, NT], F32, KC)
    bonus = decbufs[2]
    gate = agbuf
    wc = g.sb("wc", [128, KC, 2], F32, KC)
    yT = hb
    gact = g.sb("gact", [128, FC, NT], BF16, FC)
    carry = [g.sb("carry%d" % l, [128, KC], F32) for l in range(2)]
    Sf = [g.sb("Sf%d" % l, [128, KC, 64], F32) for l in range(2)]
    Sb = [g.sb("Sb%d" % l, [128, KC, 128], BF16) for l in range(2)]
    sst = g.sb("sst", [64, 16, 64], F32)
    TMP = [g.sb("tmp%d" % i, [128, max(NT, 128)], F32) for i in range(11)]
    tmpi = [0]

    def tmp():
        tmpi[0] = (tmpi[0] + 1) % len(TMP)
        return TMP[tmpi[0]]

    TB = [g.sb("tb%d" % i, [128, NT], BF16) for i in range(8)]
    tbi = [0]

    def tmpb():
        tbi[0] = (tbi[0] + 1) % len(TB)
        return TB[tbi[0]]

    XR = [g.sb("XR%d" % i, [128, 4, 192], F32, 2) for i in range(2)]
    XTb = [g.sb("XTb%d" % i, [128, 4, 128], F32, 2) for i in range(2)]
    AKTb = g.sb("AKTb", [128, 4, 128], BF16)
    ARBb = g.sb("ARBb", [128, 4, 128], BF16)
    ARKb = g.sb("ARKb", [128, 4, 128], BF16)
    Ub = g.sb("Ub", [128, 16, 64], BF16, 4)
    NSLOT = 6
    Kring = g.sb("Kring", [128, KC, NSLOT * 128], BF16, NSLOT)
    Vring = g.sb("Vring", [128, NSLOT, D], BF16, NSLOT)
    qT = rT
    oT = kT
    kvst = g.sb("kvst", [128, NBLK, D], F32, NBLK)
    PT = [g.sb("PT%d" % i, [128, 5, 128], BF16) for i in range(2)]
    biasb = [g.sb("biasb%d" % i, [128, 5, 128], BF16) for i in range(2)]
    BIASE = g.dram("BIASE", [32, 128, 768], F32)
    BIASB = g.dram("BIASB", [32, 128, 5 * 128], BF16, 32)
    NWS, NWB, WPC = 5, 5, 704
    WST = [g.sb("wst%d" % i, [128, WPC], F32) for i in range(NWS)]
    WBF = [g.sb("wbf%d" % i, [128, 2816], BF16, 8) for i in range(NWB)]
    wi = [0]
    wsi = [0]
    biasf = Buf(T1.t[:, 0:5, :], 1)
    biash = Buf(T2.t[:, 0:5, :].bitcast(BF16)[:, :, 0:128], 1)
    erow = Buf(decbufs[0].t[0:32, :, :].rearrange("p a b -> p (a b)")[:, 0:768], 1)
    biasf.regs = T1.regs
    biash.regs = T2.regs
    erow.regs = decbufs[0].regs
    WSCR_N = 52 * 1024 * 1024
    WSCR = nc.dram_tensor("WSCR", [WSCR_N], BF16, kind="Internal").ap()
    wmemo = {}
    wpend = []
    wofs = [0]

    def V(name):
        return VIDX[name]

    def pcol(row, c):
        return pv.t[:, row, c:c + 1]

    def wslab(src3):
        P, a, b = src3.shape
        key = (src3.tensor.name, int(src3.offset), tuple(tuple(x) for x in src3.ap))
        if key in wmemo and wpend:
            wpend.pop()()
        bf = WBF[wi[0] % NWB]
        wi[0] += 1
        dstb = bf.t[0:P, 0:a * b].rearrange("p (a b) -> p a b", a=a)
        if key in wmemo:
            scr, reg = wmemo[key]
            g.dma(bf, bf.t[0:P, 0:a * b], scr, [reg], bf.rg())
            return dstb, bf.rg()
        if a > 1:
            sa = max(1, WPC // b)
            assert b <= WPC
            pieces = [(a0, min(a, a0 + sa), 0, b) for a0 in range(0, a, sa)]
        else:
            pieces = [(0, 1, b0, min(b, b0 + WPC)) for b0 in range(0, b, WPC)]
        for pidx, (a0, a1, b0, b1) in enumerate(pieces):
            stg = WST[wsi[0] % NWS]
            cen = ("dve", "act", "pool", "dve")[wsi[0] % 4]
            wsi[0] += 1
            dst = stg.t[0:P, 0:(a1 - a0) * (b1 - b0)].rearrange("p (a b) -> p a b", a=a1 - a0)
            g.dma(stg, dst, src3[:, a0:a1, b0:b1], [], stg.rg())
            g.copy(cen, dstb[:, a0:a1, b0:b1], dst, stg.rg(), bf.rg(min(pidx, 7)))
        scr = bass.AP(WSCR.tensor, wofs[0], [[a * b, P], [1, a * b]])
        wofs[0] += P * a * b
        assert wofs[0] <= WSCR_N
        reg = Reg()
        if wpend:
            wpend.pop()()
        wpend.append(lambda: g.dma(bf, scr, bf.t[0:P, 0:a * b], bf.rg(), [reg]))
        wmemo[key] = (scr, reg)
        return dstb, bf.rg()

    psr = [0]

    def pregion(n):
        psr[0] = (psr[0] + 1) % 4
        b = PS[4 + psr[0]]
        return b.t[:, 0:n], b.rg()

    def proj(W, xs, n, sink, M=None):
        K, Mw = W.shape
        M = Mw
        full = all(nr == 128 for (_, _, nr) in xs)
        if full:
            npc = len(xs)
            mper = max(1, min((2816 // npc) // 128, (M + 127) // 128))
            if M < 128:
                slabs = [(0, M)]
            else:
                slabs = [(m0, min(mper * 128, M - m0)) for m0 in range(0, M, mper * 128)]
            for (m0, mc) in slabs:
                src = W[:, m0:m0 + mc].rearrange("(kc p) m -> p kc m", p=128)
                wb, wregs = wslab(src)
                for mo in range(0, mc, 128):
                    mw = min(128, mc - mo)
                    pa, pregs = pregion(n)
                    items = []
                    rd = list(wregs)
                    for ki, (xa, xr, nr) in enumerate(xs):
                        items.append((pa[0:mw, :], wb[:, ki, mo:mo + mw], xa, ki == 0, ki == npc - 1))
                        rd += xr
                    g.mm(items, rd, pregs)
                    sink((m0 + mo) // 128, mw, pa, pregs)
        else:
            wbs = []
            r0 = 0
            for (xa, xr, nr) in xs:
                src = W[r0:r0 + nr, :].rearrange("p (a m) -> p a m", a=1)
                wbs.append(wslab(src))
                r0 += nr
            for m0 in range(0, M, 128):
                mw = min(128, M - m0)
                pa, pregs = pregion(n)
                items = []
                rd = []
                for ki, (xa, xr, nr) in enumerate(xs):
                    wb, wregs = wbs[ki]
                    items.append((pa[0:mw, :], wb[:, 0, m0:m0 + mw], xa, ki == 0, ki == len(xs) - 1))
                    rd += xr + wregs
                g.mm(items, rd, pregs)
                sink(m0 // 128, mw, pa, pregs)

    def xs_of(buf, n):
        return [(buf.t[:, c, 0:n], buf.rg(c), 128) for c in range(KC)]

    e = None
    g.op("pool", lambda e: e.memset(ident.t[:], 1.0), [], ident.rg())
    g.op("pool", lambda e: e.affine_select(out=ident.t[:], in_=ident.t[:], pattern=[[-1, 128]], compare_op=ALU.is_equal,
                                           fill=0.0, base=0, channel_multiplier=1), ident.rg(), ident.rg())
    g.op("dve", lambda e: e.tensor_copy(out=identb.t[:], in_=ident.t[:]), ident.rg(), identb.rg())
    g.op("dve", lambda e: e.memset(onesm.t[:], 1.0 / 1024.0), [], onesm.rg())
    g.op("dve", lambda e: e.memset(ones1.t[:], 1.0), [], ones1.rg())
    for (ob, val) in ((onesb, 1.0), (onesb64, 1.0 / 64.0)):
        g.op("dve", lambda e: e.memset(ob.t[:], 0.0), [], ob.rg())
        g.op("dve", lambda e: e.memset(ob.t[0:64, 0:64], val), [], ob.rg())
        g.op("dve", lambda e: e.memset(ob.t[64:128, 64:128], val), [], ob.rg())
    for (mb, pat, cm, cop) in ((m_sl, -1, 1, ALU.is_gt), (m_su, 1, -1, ALU.is_gt), (m_iu, 1, -1, ALU.is_ge)):
        g.op("pool", lambda e: e.memset(mb.t[:], 1.0), [], mb.rg())
        g.op("pool", lambda e: e.affine_select(out=mb.t[:], in_=mb.t[:], pattern=[[pat, 128]], compare_op=cop,
                                               fill=0.0, base=0, channel_multiplier=cm), mb.rg(), mb.rg())
    g.op("dve", lambda e: e.memset(epsb.t[:, 0:1], 1e-6), [], epsb.rg())
    g.op("dve", lambda e: e.memset(epsb.t[:, 1:2], 64e-5), [], epsb.rg())
    g.op("dve", lambda e: e.memset(epsb.t[:, 2:3], 1e-24), [], epsb.rg())
    g.op("dve", lambda e: e.memset(epsb.t[:, 3:4], 0.0), [], epsb.rg())
    for l in range(2):
        g.op("dve", lambda e: e.memset(Sb[l].t[:], 0.0), [], Sb[l].rg())
    for eb in (aT2, bT2, rT2):
        g.op("pool", lambda e: e.memset(eb.t[:], 0.0), [], eb.rg())
    for eb in (XR[0], XR[1]):
        g.op("pool", lambda e: e.memset(eb.t[:], 0.0), [], eb.rg())

    def expand(src, dst, mi, n):
        g.op("pool", lambda e: e.tensor_copy(out=dst.t[0:64, mi, 0, 0:n], in_=src.t[0:64, mi, 0:n]), src.rg(mi), dst.rg(mi))
        g.op("pool", lambda e: e.tensor_copy(out=dst.t[64:128, mi, 1, 0:n], in_=src.t[64:128, mi, 0:n]), src.rg(mi), dst.rg(mi))

    VIDX = {}
    rp = [0]

    def addrows(name, ap2, n):
        VIDX[name] = rp[0]
        g.dma(rows, rows.t[rp[0]:rp[0] + n, :], ap2, [], rows.rg())
        rp[0] += n

    g.op("dve", lambda e: e.memset(rows.t[:], 0.0), [], rows.rg())
    addrows("norm_mix", I["norm_mix"], 4)
    addrows("norm_ffn", I["norm_ffn"], 4)
    addrows("rw_mix", I["rw_mix"].rearrange("l s d -> (l s) d"), 12)
    for nm in ("rw_w0", "rw_a0", "rw_kk", "rw_ka", "rw_rk", "rw_lnw", "rw_lnb"):
        addrows(nm, I[nm], 2)
    addrows("rw_v0", I["rw_v0"], 1)
    addrows("kv_norm", I["kv_norm"].rearrange("(a d) -> a d", a=1), 1)
    addrows("b_ada", I["b_ada"].rearrange("l (s d) -> (l s) d", s=6), 24)
    addrows("kv_ada_b", I["kv_ada_b"].rearrange("(s d) -> s d", s=2), 2)
    addrows("c_prompt", I["c_prompt"], 2)
    addrows("c_sample", I["c_sample"], 1)
    addrows("state_shift", I["state_shift"].rearrange("l b d -> (l b) d"), 2)
    VIDX["k_norm"] = rp[0]
    kn = I["k_norm"]
    g.dma(rows, rows.t[rp[0]:rp[0] + 1, :].rearrange("p (h j) -> p h j", h=16),
          bass.AP(kn.tensor, 0, [[0, 1], [0, 16], [1, 64]]), [], rows.rg())
    rp[0] += 1
    VIDX["q_norm"] = rp[0]
    qn = I["q_norm"]
    g.dma(rows, rows.t[rp[0]:rp[0] + 2, :].rearrange("p (h j) -> p h j", h=16),
          bass.AP(qn.tensor, 0, [[64, 2], [0, 16], [1, 64]]), [], rows.rg())
    rp[0] += 2
    assert rp[0] <= NV
    for c in range(KC):
        pb = PS[6 + c % 2]
        g.tr([(pb.t[:, 0:NV], rows.t[0:NV, c * 128:(c + 1) * 128], ident.t[0:NV, 0:NV])], rows.rg() + ident.rg(), pb.rg(0, 1))
        g.op("dve", lambda e: e.tensor_copy(out=pv.t[:, :, c], in_=pb.t[:, 0:NV]), pb.rg(0, 1), pv.rg())
    ck(1)
    cr = V("c_prompt")
    g.op("act", lambda e: e.activation(out=scT.t[:, :, 0:3], in_=pv.t[:, cr:cr + 3, :].rearrange("p s c -> p c s"),
                                       func=AF.Silu), pv.rg(), scT.rg())

    rb = I["rel_bias"].rearrange("j h r -> (j h) r")
    g.dma(erow, erow.t[:, 0:256], rb[:, 1:257], [], erow.rg())
    g.op("dve", lambda e: e.tensor_copy(out=erow.t[:, 256:768], in_=erow.t[:, 255:256].to_broadcast([32, 512])),
         erow.rg(), erow.rg())
    g.dma(erow, BIASE.t[:, :, :], erow.t[:, :].unsqueeze(1).to_broadcast([32, 128, 768]), erow.rg(), BIASE.rg())

    def bias_steps():
        for jh in range(32):
            src = bass.AP(BIASE.t.tensor, jh * 128 * 768 + 127, [[767, 128], [128, 5], [1, 128]])
            g.dma(biasf, biasf.t[:], src, BIASE.rg(), biasf.rg())
            g.op("dve", lambda e: e.memset(biasf.t[0:64, 4, 64:128], NEG), biasf.rg(), biasf.rg())
            g.op("dve", lambda e: e.memset(biasf.t[64:128, 0, 0:64], NEG), biasf.rg(), biasf.rg())
            g.op("dve", lambda e: e.tensor_copy(out=biash.t[:], in_=biasf.t[:]), biasf.rg(), biash.rg())
            g.dma(biash, BIASB.t[jh].rearrange("p (t q) -> p t q", t=5), biash.t[:], biash.rg(), BIASB.rg(jh))
            yield
    bias_gen = bias_steps()

    adacnt = [0]
    ADAST = [decbufs[1], decbufs[2], agbuf, kfbuf, hb, xx, vfirst]

    def ada(Wl, ncols, outv, brow):
        for j in range(ncols // 1024):
            for hf in range(2):
                pb = PS[6 + hf]
                for q in range(4):
                    m = j * 8 + hf * 4 + q
                    stg = ADAST[adacnt[0] % len(ADAST)]
                    adacnt[0] += 1
                    g.dma(stg, stg.t[:, :, 0:128], Wl[:, m * 128:(m + 1) * 128].rearrange("(kc p) m -> p kc m", p=128), [], stg.rg())
                    items = [(pb.t[0:3, q * 128:(q + 1) * 128], scT.t[:, kc, 0:3], stg.t[:, kc, 0:128], kc == 0, kc == KC - 1)
                             for kc in range(KC)]
                    g.mm(items, stg.rg() + scT.rg(), pb.rg())
                    if adacnt[0] % 6 == 5:
                        next(bias_gen, None)
                g.copy("act", xin.t[0:3, 0, hf * 512:(hf + 1) * 512], pb.t[0:3, :], pb.rg(), xin.rg(0))
            pt_ = PS[5]
            g.tr([(pt_.t[:, c * 3:(c + 1) * 3], xin.t[0:3, 0, c * 128:(c + 1) * 128], ident.t[0:3, 0:3]) for c in range(KC)],
                 xin.rg(0) + ident.rg(), pt_.rg())
            g.op("dve", lambda e: e.tensor_tensor(out=outv[:, j * 8:(j + 1) * 8, :], in0=pt_.t[:, 0:24].rearrange("p (c s) -> p c s", s=3),
                                                  in1=pv.t[:, brow + j, :].unsqueeze(2).to_broadcast([128, KC, 3]), op=ALU.add),
                 pt_.rg() + pv.rg(), mod.rg() + kvm.rg())

    for l in range(NL):
        ada(I["w_ada"][l], 6 * D, mod.t[:, l], V("b_ada") + 6 * l)
    ada(I["kv_ada_w"], 2 * D, kvm.t[:], V("kv_ada_b"))
    for _ in bias_gen:
        pass
    for l in range(NL):
        for sub in range(2):
            j = 1 if sub == 0 else 4
            grow = (V("norm_mix") if sub == 0 else V("norm_ffn")) + l
            for s in range(3):
                g.op("dve", lambda e: e.scalar_tensor_tensor(
                    out=gm.t[:, (l * 2 + sub) * 3 + s, :], in0=mod.t[:, l, j * 8:j * 8 + 8, s], scalar=1.0,
                    in1=pv.t[:, grow, :], op0=ALU.add, op1=ALU.mult), mod.rg() + pv.rg(), gm.rg())
    for s in range(3):
        g.op("dve", lambda e: e.scalar_tensor_tensor(
            out=gkv.t[:, s, :], in0=kvm.t[:, 8:16, s], scalar=1.0, in1=pv.t[:, V("kv_norm"), :],
            op0=ALU.add, op1=ALU.mult), kvm.rg() + pv.rg(), gkv.rg())

    ck(2)
    ck(3)
    def load_xT(src_tok, n, dstbuf, dst_dt_regs=None):
        nb = (n + 127) // 128
        for b in range(nb):
            nn = min(128, n - b * 128)
            g.dma(xin, xin.t[0:nn, b, :], src_tok[b * 128:b * 128 + nn, :], [], xin.rg(b))
        ck(32)
        for c in range(KC):
            pa, pregs = pregion(n)
            items = []
            for b in range(nb):
                nn = min(128, n - b * 128)
                items.append((pa[:, b * 128:b * 128 + nn], xin.t[0:nn, b, c * 128:(c + 1) * 128], ident.t[0:nn, 0:nn]))
            g.tr(items, xin.rg() + ident.rg(), pregs)
            ck(33)
            g.copy(g.ev_eng(), dstbuf.t[:, c, 0:n], pa[:, 0:n], pregs, dstbuf.rg(c))
            ck(34)

    def store_tok(srcbuf, n, dst_tok, colofs=0, is_out=True, src_ap=None):
        nb = (n + 127) // 128
        for b in range(nb):
            nn = min(128, n - b * 128)
            for half in range(2):
                pb = PS[6 + half]
                items = []
                for cc in range(4):
                    c = half * 4 + cc
                    items.append((pb.t[0:nn, cc * 128:(cc + 1) * 128],
                                  srcbuf.t[:, c, colofs + b * 128:colofs + b * 128 + nn], ident.t[:, :]))
                g.tr(items, srcbuf.rg() + ident.rg(), pb.rg())
                g.copy(g.ev_eng(), xin.t[0:nn, b, half * 512:(half + 1) * 512], pb.t[0:nn, :], pb.rg(), xin.rg(b))
            g.dma(xin, dst_tok[b * 128:b * 128 + nn, :], xin.t[0:nn, b, :], xin.rg(b), [], is_out=is_out)

    def rmsnorm_mod(src, n, gmrow_ap, shift_ap_fn, out_f32=None, out_bf=None):
        sq = tmpb_big[0]
        g.op("act", lambda e: e.activation(out=sq.t[:, :, 0:n], in_=src.t[:, :, 0:n], func=AF.Square), src.rg(), sq.rg())
        pa, pregs = pregion(n)
        g.mm([(pa, onesm.t[:, :], sq.t[:, c, 0:n], c == 0, c == KC - 1) for c in range(KC)], sq.rg() + onesm.rg(), pregs)
        rs = tmp()
        g.op("act", lambda e: e.activation(out=rs.t[:, 0:n], in_=pa, func=AF.Ln, bias=epsb.t[:, 0:1], scale=1.0),
             pregs + epsb.rg(), rs.rg())
        g.op("act", lambda e: e.activation(out=rs.t[:, 0:n], in_=rs.t[:, 0:n], func=AF.Exp, scale=-0.5), rs.rg(), rs.rg())
        for c in range(KC):
            t1 = tmp()
            g.op("dve", lambda e: e.scalar_tensor_tensor(out=t1.t[:, 0:n], in0=src.t[:, c, 0:n], scalar=gmrow_ap[:, c:c + 1],
                                                         in1=rs.t[:, 0:n], op0=ALU.mult, op1=ALU.mult),
                 src.rg(c) + rs.rg() + gm.rg() + gkv.rg(), t1.rg())
            sh = shift_ap_fn(c)
            if out_f32 is not None:
                g.op("act", lambda e: e.activation(out=out_f32.t[:, c, 0:n], in_=t1.t[:, 0:n], func=AF.Identity, bias=sh, scale=1.0),
                     t1.rg() + mod.rg() + kvm.rg(), out_f32.rg(c))
            if out_bf is not None:
                g.op("act", lambda e: e.activation(out=out_bf.t[:, c, 0:n], in_=t1.t[:, 0:n], func=AF.Identity, bias=sh, scale=1.0),
                     t1.rg() + mod.rg() + kvm.rg(), out_bf.rg(c))

    tmpb_big = [aT]

    def ffn(l, s, n):
        gmrow = gm.t[:, (l * 2 + 1) * 3 + s, :]
        rmsnorm_mod(xT, n, gmrow, lambda c: mod.t[:, l, 3 * 8 + c, s:s + 1], out_bf=hbf)
        xs = xs_of(hbf, n)
        gts = {}

        def sink_gate(mi, mw, pa, pregs):
            t = tmp()
            g.op("act", lambda e: e.activation(out=t.t[:, 0:n], in_=pa, func=AF.Silu), pregs, t.rg())
            gts[mi] = t

        def sink_up(mi, mw, pa, pregs):
            t = gts[mi]
            g.op("dve", lambda e: e.tensor_tensor(out=gact.t[:, mi, 0:n], in0=pa, in1=t.t[:, 0:n], op=ALU.mult),
                 pregs + t.rg(), gact.rg(mi))

        Wg, Wu = I["w_gate"][l], I["w_up"][l]
        for m0 in range(0, FF, 256):
            proj(Wg[:, m0:m0 + 256], xs, n, lambda mi, mw, pa, pr, m0=m0: sink_gate(m0 // 128 + mi, mw, pa, pr))
            proj(Wu[:, m0:m0 + 256], xs, n, lambda mi, mw, pa, pr, m0=m0: sink_up(m0 // 128 + mi, mw, pa, pr))
        xs2 = [(gact.t[:, c, 0:n], gact.rg(c), 128) for c in range(FC)]

        def sink_down(mi, mw, pa, pregs):
            g.op("dve", lambda e: e.scalar_tensor_tensor(out=xT.t[:, mi, 0:n], in0=pa, scalar=mod.t[:, l, 5 * 8 + mi, s:s + 1],
                                                         in1=xT.t[:, mi, 0:n], op0=ALU.mult, op1=ALU.add),
                 pregs + xT.rg(mi) + mod.rg(), xT.rg(mi))

        proj(I["w_down"][l], xs2, n, sink_down)

    def resid_sink(l, s, n):
        def sink(mi, mw, pa, pregs):
            g.op("dve", lambda e: e.scalar_tensor_tensor(out=xT.t[:, mi, 0:n], in0=pa, scalar=mod.t[:, l, 2 * 8 + mi, s:s + 1],
                                                         in1=xT.t[:, mi, 0:n], op0=ALU.mult, op1=ALU.add),
                 pregs + xT.rg(mi) + mod.rg(), xT.rg(mi))
        return sink

    def to_tm(src_bf_ap, src_regs, n, dstbuf, c):
        nb = (n + 127) // 128
        pa, pregs = pregion(128 * nb)
        pab = pa.bitcast(BF16)
        items = []
        for b in range(nb):
            nn = min(128, n - b * 128)
            items.append((pab[0:nn, b * 128:(b + 1) * 128], src_bf_ap[:, b * 128:b * 128 + nn], identb.t[:, :]))
        g.tr(items, src_regs + identb.rg(), pregs)
        for b in range(nb):
            nn = min(128, n - b * 128)
            g.copy(g.ev_eng(), dstbuf.t[0:nn, b, c * 128:(c + 1) * 128], pab[0:nn, b * 128:(b + 1) * 128], pregs, dstbuf.rg(b))

    def bc(ap2, n):
        return ap2.unsqueeze(2).to_broadcast([128, KC, n])

    def prow(row):
        return pv.t[:, row, :]

    def headsum(src_bf, n, lhs, evac):
        for hf in range(2):
            pa, pregs = pregion(4 * n)
            items = [(pa[:, i * n:(i + 1) * n], lhs.t[:, :], src_bf.t[:, hf * 4 + i, 0:n], True, True) for i in range(4)]
            g.mm(items, src_bf.rg() + lhs.rg(), pregs)
            evac(hf * 4, pa.rearrange("p (a b) -> p a b", a=4), pregs)

    def rwkv(l, s, n, first_tile):
        gmrow = gm.t[:, (l * 2 + 0) * 3 + s, :]
        rmsnorm_mod(xT, n, gmrow, lambda c: mod.t[:, l, 0 * 8 + c, s:s + 1], out_f32=hb)
        if n > 1:
            g.op("dve", lambda e: e.tensor_tensor(out=xx.t[:, :, 1:n], in0=hb.t[:, :, 0:n - 1], in1=hb.t[:, :, 1:n], op=ALU.subtract),
                 hb.rg(), xx.rg())
        g.op("dve", lambda e: e.tensor_tensor(out=xx.t[:, :, 0], in0=carry[l].t[:, :], in1=hb.t[:, :, 0], op=ALU.subtract),
             hb.rg() + carry[l].rg(), xx.rg())
        g.op("dve", lambda e: e.tensor_copy(out=carry[l].t[:, :], in_=hb.t[:, :, n - 1]), hb.rg(), carry[l].rg())
        mixrow = V("rw_mix") + 6 * l
        mixcnt = [0]

        def mix(i):
            b = xm[mixcnt[0] % 2]
            mixcnt[0] += 1
            for c in range(KC):
                en = "dve" if c % 4 != 3 else "dve"
                g.op(en, lambda e: e.scalar_tensor_tensor(out=b.t[:, c, 0:n], in0=xx.t[:, c, 0:n], scalar=pcol(mixrow + i, c),
                                                          in1=hb.t[:, c, 0:n], op0=ALU.mult, op1=ALU.add),
                     xx.rg(c) + hb.rg(c) + pv.rg(), b.rg(c))
            return b, xs_of(b, n)

        nch = (n + 127) // 128
        csz = [min(128, n - k * 128) for k in range(nch)]
        Wb, IWb, WPb = decbufs[0], decbufs[1], decbufs[2]
        ag_b, kf_b = agbuf, kfbuf
        A3 = lambda bf_: bf_.t[:, :, 0:n]

        _, xs = mix(1)
        t1 = tmpb()

        def sink_w1(mi, mw, pa, pregs):
            g.op("act", lambda e: e.activation(out=t1.t[0:64, 0:n], in_=pa[0:64, :], func=AF.Tanh), pregs, t1.rg())
        proj(I["rw_w1"][l], xs, n, sink_w1)

        def sink_w2(mi, mw, pa, pregs):
            g.op("act", lambda e: e.activation(out=WPb.t[:, mi, 0:n], in_=pa, func=AF.Sigmoid, bias=pcol(V("rw_w0") + l, mi), scale=1.0),
                 pregs + pv.rg(), WPb.rg(mi))
            for k in range(nch):
                a_, b_ = k * 128, k * 128 + csz[k]
                g.op("dve", lambda e: e.tensor_tensor_scan(out=IWb.t[:, mi, a_:b_], data0=WPb.t[:, mi, a_:b_], data1=WPb.t[:, mi, a_:b_],
                                                           initial=0.0, op0=ALU.add, op1=ALU.bypass), WPb.rg(mi), IWb.rg(mi))
        proj(I["rw_w2"][l], [(t1.t[0:64, 0:n], t1.rg(), 64)], n, sink_w2)
        _, xs = mix(4)
        t2 = tmpb()

        def sink_a1(mi, mw, pa, pregs):
            g.copy("act", t2.t[0:64, 0:n], pa[0:64, :], pregs, t2.rg())
        proj(I["rw_a1"][l], xs, n, sink_a1)

        def sink_a2(mi, mw, pa, pregs):
            g.op("act", lambda e: e.activation(out=ag_b.t[:, mi, 0:n], in_=pa, func=AF.Sigmoid, bias=pcol(V("rw_a0") + l, mi), scale=1.0),
                 pregs + pv.rg(), ag_b.rg(mi))
        proj(I["rw_a2"][l], [(t2.t[0:64, 0:n], t2.rg(), 64)], n, sink_a2)

        g.op("dve", lambda e: e.tensor_tensor(out=A3(WPb), in0=A3(IWb), in1=A3(WPb), op=ALU.subtract), WPb.rg() + IWb.rg(), WPb.rg())
        g.op("act", lambda e: e.activation(out=A3(WPb), in_=A3(WPb), func=AF.Exp, scale=-C0), WPb.rg(), WPb.rg())
        g.op("act", lambda e: e.activation(out=A3(Wb), in_=A3(IWb), func=AF.Exp, scale=-C0), IWb.rg(), Wb.rg())
        g.op("act", lambda e: e.activation(out=A3(IWb), in_=A3(IWb), func=AF.Exp, scale=C0), IWb.rg(), IWb.rg())
        for k in range(nch):
            g.op("pool", lambda e: e.tensor_copy(out=wc.t[:, :, k], in_=Wb.t[:, :, k * 128 + csz[k] - 1]), Wb.rg(), wc.rg())

        xk, xs = mix(2)

        def sink_k(mi, mw, pa, pregs):
            g.copy(g.ev_eng(), kf_b.t[:, mi, 0:n], pa, pregs, kf_b.rg(mi))
        proj(I["rw_k"][l], xs, n, sink_k)
        g.op("dve", lambda e: e.tensor_tensor(out=A3(T1), in0=A3(kf_b), in1=bc(prow(V("rw_kk") + l), n), op=ALU.mult),
             kf_b.rg() + pv.rg(), T1.rg())
        sqk = xk
        g.op("act", lambda e: e.activation(out=A3(sqk), in_=A3(T1), func=AF.Square), T1.rg(), sqk.rg())

        def ev_kn(c0, p3, pregs):
            g.op("dve", lambda e: e.tensor_scalar(out=T2.t[:, c0:c0 + 4, 0:n], in0=p3, scalar1=1e-24, scalar2=None, op0=ALU.max),
                 pregs, T2.rg())
        headsum(sqk, n, onesb, ev_kn)
        g.op("act", lambda e: e.activation(out=A3(T2), in_=A3(T2), func=AF.Ln), T2.rg(), T2.rg())
        g.op("act", lambda e: e.activation(out=A3(T2), in_=A3(T2), func=AF.Exp, scale=-0.5), T2.rg(), T2.rg())
        g.op("dve", lambda e: e.tensor_tensor(out=A3(T1), in0=A3(T1), in1=A3(T2), op=ALU.mult), T1.rg() + T2.rg(), T1.rg())
        g.op("dve", lambda e: e.scalar_tensor_tensor(out=A3(T2), in0=A3(ag_b), scalar=-1.0, in1=bc(prow(V("rw_ka") + l), n),
                                                     op0=ALU.add, op1=ALU.mult), ag_b.rg() + pv.rg(), T2.rg())
        g.op("dve", lambda e: e.scalar_tensor_tensor(out=A3(kf_b), in0=A3(T2), scalar=1.0, in1=A3(kf_b), op0=ALU.add, op1=ALU.mult),
             T2.rg() + kf_b.rg(), kf_b.rg())
        g.op("dve", lambda e: e.scalar_tensor_tensor(out=A3(aT), in0=A3(T1), scalar=-1.0, in1=A3(WPb), op0=ALU.mult, op1=ALU.mult),
             T1.rg() + WPb.rg(), aT.rg())
        g.op("dve", lambda e: e.tensor_tensor(out=A3(T2), in0=A3(T1), in1=A3(ag_b), op=ALU.mult), T1.rg() + ag_b.rg(), T2.rg())
        g.op("dve", lambda e: e.tensor_tensor(out=A3(bT), in0=A3(T2), in1=A3(IWb), op=ALU.mult), T2.rg() + IWb.rg(), bT.rg())
        g.op("dve", lambda e: e.tensor_tensor(out=A3(kT), in0=A3(kf_b), in1=A3(IWb), op=ALU.mult), kf_b.rg() + IWb.rg(), kT.rg())
        for (src_, dst_) in ((aT, aT2), (bT, bT2)):
            g.copy("act", dst_.t[0:64, :, 0, 0:n], src_.t[0:64, :, 0:n], src_.rg(), dst_.rg())
            g.copy("act", dst_.t[64:128, :, 1, 0:n], src_.t[64:128, :, 0:n], src_.rg(), dst_.rg())
        for mi in range(KC):
            to_tm(bT.t[:, mi, 0:n], bT.rg(mi), n, btm, mi)
            to_tm(kT.t[:, mi, 0:n], kT.rg(mi), n, ktm, mi)

        xr, xs = mix(0)

        def sink_r(mi, mw, pa, pregs):
            g.copy(g.ev_eng(), T1.t[:, mi, 0:n], pa, pregs, T1.rg(mi))
        proj(I["rw_r"][l], xs, n, sink_r)
        g.op("dve", lambda e: e.tensor_tensor(out=A3(rT), in0=A3(T1), in1=A3(Wb), op=ALU.mult), T1.rg() + Wb.rg(), rT.rg())
        g.copy("act", rT2.t[0:64, :, 0, 0:n], rT.t[0:64, :, 0:n], rT.rg(), rT2.rg())
        g.copy("act", rT2.t[64:128, :, 1, 0:n], rT.t[64:128, :, 0:n], rT.rg(), rT2.rg())
        g.op("dve", lambda e: e.tensor_tensor(out=A3(T1), in0=A3(T1), in1=bc(prow(V("rw_rk") + l), n), op=ALU.mult), T1.rg() + pv.rg(), T1.rg())
        rkb = xr
        g.op("dve", lambda e: e.tensor_tensor(out=A3(rkb), in0=A3(T1), in1=A3(kf_b), op=ALU.mult), T1.rg() + kf_b.rg(), rkb.rg())

        def ev_rk(c0, p3, pregs):
            g.copy("act", bonus.t[:, c0:c0 + 4, 0:n], p3, pregs, bonus.rg())
        headsum(rkb, n, onesb, ev_rk)

        xv, xs = mix(3)
        if l > 0:
            t4 = tmpb()

            def sink_v1(mi, mw, pa, pregs):
                g.copy("act", t4.t[0:32, 0:n], pa[0:32, :], pregs, t4.rg())
            proj(I["rw_v1"][0], xs, n, sink_v1)

            def sink_v2(mi, mw, pa, pregs):
                g.op("act", lambda e: e.activation(out=T2.t[:, mi, 0:n], in_=pa, func=AF.Sigmoid, bias=pcol(V("rw_v0"), mi), scale=1.0),
                     pregs + pv.rg(), T2.rg(mi))
            proj(I["rw_v2"][0], [(t4.t[0:32, 0:n], t4.rg(), 32)], n, sink_v2)
        vdst = vfirst if l == 0 else T1

        def sink_v(mi, mw, pa, pregs):
            g.copy(g.ev_eng(), vdst.t[:, mi, 0:n], pa, pregs, vdst.rg(mi))
        proj(I["rw_v"][l], xs, n, sink_v)
        if l > 0:
            g.op("dve", lambda e: e.tensor_tensor(out=A3(kf_b), in0=A3(vfirst), in1=A3(T1), op=ALU.subtract), vfirst.rg() + T1.rg(), kf_b.rg())
            g.op("dve", lambda e: e.tensor_tensor(out=A3(kf_b), in0=A3(kf_b), in1=A3(T2), op=ALU.mult), kf_b.rg() + T2.rg(), kf_b.rg())
            g.op("dve", lambda e: e.tensor_tensor(out=A3(T1), in0=A3(T1), in1=A3(kf_b), op=ALU.add), T1.rg() + kf_b.rg(), T1.rg())
        vb = xv
        g.copy("act", A3(vb), A3(vdst), vdst.rg(), vb.rg())
        g.op("dve", lambda e: e.tensor_tensor(out=A3(bonus), in0=A3(bonus), in1=A3(vdst), op=ALU.mult), bonus.rg() + vdst.rg(), bonus.rg())
        for mi in range(KC):
            to_tm(vb.t[:, mi, 0:n], vb.rg(mi), n, vtm, mi)

        _, xs = mix(5)
        t5 = [tmpb(), tmpb()]

        def sink_g1(mi, mw, pa, pregs):
            g.op("act", lambda e: e.activation(out=t5[mi].t[0:mw, 0:n], in_=pa[0:mw, :], func=AF.Sigmoid), pregs, t5[mi].rg())
        proj(I["rw_g1"][l], xs, n, sink_g1)

        def sink_g2(mi, mw, pa, pregs):
            g.copy("act", gate.t[:, mi, 0:n], pa, pregs, gate.rg(mi))
        proj(I["rw_g2"][l], [(t5[0].t[:, 0:n], t5[0].rg(), 128), (t5[1].t[0:32, 0:n], t5[1].rg(), 32)], n, sink_g2)

        for k in range(nch):
            scan_chunk(l, k, csz[k])

        yb, ysq = xm[0], xm[1]
        g.copy("act", A3(yb), A3(yT), yT.rg(), yb.rg())

        def ev_mean(c0, p3, pregs):
            g.op("dve", lambda e: e.tensor_tensor(out=T1.t[:, c0:c0 + 4, 0:n], in0=yT.t[:, c0:c0 + 4, 0:n], in1=p3, op=ALU.subtract),
                 yT.rg() + pregs, T1.rg())
        headsum(yb, n, onesb64, ev_mean)
        g.op("act", lambda e: e.activation(out=A3(ysq), in_=A3(T1), func=AF.Square), T1.rg(), ysq.rg())

        def ev_var(c0, p3, pregs):
            g.op("act", lambda e: e.activation(out=T2.t[:, c0:c0 + 4, 0:n], in_=p3, func=AF.Ln, bias=epsb.t[:, 1:2], scale=1.0),
                 pregs + epsb.rg(), T2.rg())
        headsum(ysq, n, onesb64, ev_var)
        g.op("act", lambda e: e.activation(out=A3(T2), in_=A3(T2), func=AF.Exp, scale=-0.5), T2.rg(), T2.rg())
        g.op("dve", lambda e: e.tensor_tensor(out=A3(T1), in0=A3(T1), in1=A3(T2), op=ALU.mult), T1.rg() + T2.rg(), T1.rg())
        g.op("dve", lambda e: e.tensor_tensor(out=A3(T1), in0=A3(T1), in1=bc(prow(V("rw_lnw") + l), n), op=ALU.mult), T1.rg() + pv.rg(), T1.rg())
        g.op("dve", lambda e: e.tensor_tensor(out=A3(T1), in0=A3(T1), in1=bc(prow(V("rw_lnb") + l), n), op=ALU.add), T1.rg() + pv.rg(), T1.rg())
        g.op("dve", lambda e: e.tensor_tensor(out=A3(T1), in0=A3(T1), in1=A3(bonus), op=ALU.add), T1.rg() + bonus.rg(), T1.rg())
        g.op("dve", lambda e: e.tensor_tensor(out=A3(hbf), in0=A3(T1), in1=A3(gate), op=ALU.mult), T1.rg() + gate.rg(), hbf.rg())
        proj(I["rw_o"][l], xs_of(hbf, n), n, resid_sink(l, s, n))

    def scan_chunk(l, k, nn):
        co = k * 128
        nlev = 0
        while (1 << (nlev + 1)) < nn:
            nlev += 1
        S_f, S_b = Sf[l], Sb[l]
        for grp in range(4):
            heads = [grp * 4 + i for i in range(4)]
            def hv(buf, h):
                pb_ = (h % 2) * 64
                return buf.t[pb_:pb_ + 64, h // 2, co:co + nn]
            specs = [(0, aT, bT2, m_sl, XR[0]), (1, bT, aT2, m_su, XTb[0]), (2, kT, aT2, m_su, AKTb),
                     (3, bT, rT2, m_iu, ARBb), (4, kT, rT2, m_iu, ARKb)]
            for (bk, L, Rr, msk, dst) in specs:
                items = []
                rd = []
                for pi in range(2):
                    hp = grp * 2 + pi
                    o = PS[bk].t[0:nn, pi * 256:(pi + 1) * 256].rearrange("p (a b) -> p a b", a=2)[:, :, 0:nn]
                    items.append((o, L.t[:, hp, co:co + nn], Rr.t[:, hp, :, co:co + nn], True, True))
                    rd += L.rg(hp) + Rr.rg(hp)
                g.mm(items, rd, PS[bk].rg())
                pin = PS[bk].t[0:nn, :].rearrange("p (a b) -> p a b", a=4)[:, :, 0:nn]
                dsto = dst.t[0:nn, :, 0:nn] if bk < 2 else dst.t[0:nn, :, 0:nn]
                g.op("dve", lambda e: e.tensor_tensor(out=dsto, in0=pin,
                                                      in1=msk.t[0:nn, 0:nn].unsqueeze(1).to_broadcast([nn, 4, nn]), op=ALU.mult),
                     PS[bk].rg() + msk.rg(), dst.rg())
            items = []
            rd = list(AKTb.rg()) + S_b.rg() + vtm.rg(k)
            for pi in range(2):
                hp = grp * 2 + pi
                o2 = PS[5].t[0:nn, pi * 128:(pi + 1) * 128]
                items.append((o2, aT.t[:, hp, co:co + nn], S_b.t[:, hp, :], True, False))
                rd += aT.rg(hp)
                for half in range(2):
                    h = hp * 2 + half
                    i = pi * 2 + half
                    o = PS[5].t[0:nn, i * 64:(i + 1) * 64]
                    items.append((o, AKTb.t[0:nn, i, 0:nn], vtm.t[0:nn, k, h * 64:(h + 1) * 64], False, half == 1))
            g.mm(items, rd, PS[5].rg(0, 2))
            pr = PS[5].t[0:nn, 0:256].rearrange("p (a b) -> p a b", a=4)
            g.op("act", lambda e: e.activation(out=XR[0].t[0:nn, :, 128:192], in_=pr, func=AF.Copy), PS[5].rg(0, 2), XR[0].rg())
            cur = 0
            BK = {0: (PS[5], PS[1]), 1: (PS[6], PS[7])}
            for lev in range(nlev + 1):
                XRc, XTc = XR[cur], XTb[cur]
                nxt = 1 - cur
                needX = lev < nlev - 1
                for pi in range(2):
                    bxr, bxt = BK[pi]
                    hs = (2 * pi, 2 * pi + 1)
                    if needX:
                        items = [(bxr.t[0:nn, j_ * 192:(j_ + 1) * 192], XTc.t[0:nn, i, 0:nn], XRc.t[0:nn, i, :], True, True)
                                 for j_, i in enumerate(hs)]
                    else:
                        items = [(bxr.t[0:nn, j_ * 192 + 128:(j_ + 1) * 192], XTc.t[0:nn, i, 0:nn], XRc.t[0:nn, i, 128:192], True, True)
                                 for j_, i in enumerate(hs)]
                    g.mm(items, XTc.rg(pi) + XRc.rg(pi), bxr.rg())
                    if lev < nlev:
                        items = [(bxt.t[0:nn, j_ * 128:j_ * 128 + nn], XRc.t[0:nn, i, 0:nn], XTc.t[0:nn, i, 0:nn], True, True)
                                 for j_, i in enumerate(hs)]
                        g.mm(items, XRc.rg(pi) + XTc.rg(pi), bxt.rg())
                for pi in range(2):
                    bxr, bxt = BK[pi]
                    h0 = 2 * pi
                    pv3 = bxr.t[0:nn, 0:384].rearrange("p (a b) -> p a b", a=2)
                    g.op("dve", lambda e: e.tensor_tensor(out=XR[nxt].t[0:nn, h0:h0 + 2, 128:192], in0=XRc.t[0:nn, h0:h0 + 2, 128:192],
                                                          in1=pv3[:, :, 128:192], op=ALU.add), XRc.rg(pi) + bxr.rg(), XR[nxt].rg(pi))
                    if needX:
                        g.copy("dve", XR[nxt].t[0:nn, h0:h0 + 2, 0:nn], pv3[:, :, 0:nn], bxr.rg(), XR[nxt].rg(pi))
                    if lev < nlev:
                        g.copy("act", XTb[nxt].t[0:nn, h0:h0 + 2, 0:nn],
                               bxt.t[0:nn, 0:256].rearrange("p (a b) -> p a b", a=2)[:, :, 0:nn], bxt.rg(), XTb[nxt].rg(pi))
                cur = nxt
            Rfin = XR[cur]
            g.op("act", lambda e: e.activation(out=Ub.t[0:nn, grp * 4:grp * 4 + 4, :], in_=Rfin.t[0:nn, :, 128:192], func=AF.Copy), Rfin.rg(), Ub.rg(grp))
            for pi in range(2):
                hp = grp * 2 + pi
                for half in range(2):
                    h = hp * 2 + half
                    i = pi * 2 + half
                    yreg = PS[2 + half]
                    yo = yreg.t[:, pi * 128:pi * 128 + nn]
                    items = [(yo, S_b.t[:, hp, :], rT.t[:, hp, co:co + nn], True, False),
                             (yo, Ub.t[0:nn, hp * 2:hp * 2 + 2, :].rearrange("p a b -> p (a b)"), ARBb.t[0:nn, i, 0:nn], False, False),
                             (yo, vtm.t[0:nn, k, hp * 128:(hp + 1) * 128], ARKb.t[0:nn, i, 0:nn], False, True)]
                    g.mm(items, S_b.rg() + rT.rg(hp) + Ub.rg(grp) + ARBb.rg() + ARKb.rg() + vtm.rg(k), yreg.rg(pi, pi + 1))
                    so = yreg.t[:, 256 + pi * 64:256 + (pi + 1) * 64]
                    items = [(so, btm.t[0:nn, k, hp * 128:(hp + 1) * 128], Ub.t[0:nn, h, :], True, False),
                             (so, ktm.t[0:nn, k, hp * 128:(hp + 1) * 128], vtm.t[0:nn, k, h * 64:(h + 1) * 64], False, True)]
                    g.mm(items, btm.rg(k) + ktm.rg(k) + Ub.rg(grp) + vtm.rg(k), yreg.rg(2, 3))
            for half in range(2):
                pb_ = half * 64
                yreg = PS[2 + half]
                g.copy(g.ev_eng(), yT.t[pb_:pb_ + 64, grp * 2:grp * 2 + 2, co:co + nn],
                       yreg.t[pb_:pb_ + 64, 0:256].rearrange("p (a b) -> p a b", a=2)[:, :, 0:nn], yreg.rg(0, 2), yT.rg(grp * 2, grp * 2 + 2))
                sps = yreg.t[pb_:pb_ + 64, 256:384].rearrange("p (a b) -> p a b", a=2)
                sfv = S_f.t[pb_:pb_ + 64, grp * 2:grp * 2 + 2, :]
                g.op("dve", lambda e: e.tensor_tensor(out=sfv, in0=sfv, in1=sps, op=ALU.add), S_f.rg() + yreg.rg(2, 3), S_f.rg())
                g.op("dve", lambda e: e.tensor_tensor(out=sfv, in0=sfv, in1=wc.t[pb_:pb_ + 64, grp * 2:grp * 2 + 2, k:k + 1].to_broadcast([64, 2, 64]),
                                                      op=ALU.mult), S_f.rg() + wc.rg(), S_f.rg())
                g.op("pool", lambda e: e.tensor_copy(out=S_b.t[pb_:pb_ + 64, grp * 2:grp * 2 + 2, pb_:pb_ + 64], in_=sfv), S_f.rg(), S_b.rg())

    def headnorm_all(n, gain_row, scale, out_ap3, out_regs):
        sq = xm[0]
        g.op("act", lambda e: e.activation(out=sq.t[:, :, 0:n], in_=T1.t[:, :, 0:n], func=AF.Square), T1.rg(), sq.rg())

        def ev(c0, p3, pregs):
            g.op("act", lambda e: e.activation(out=T2.t[:, c0:c0 + 4, 0:n], in_=p3, func=AF.Ln, bias=epsb.t[:, 0:1], scale=1.0),
                 pregs + epsb.rg(), T2.rg())
        headsum(sq, n, onesb64, ev)
        g.op("act", lambda e: e.activation(out=T2.t[:, :, 0:n], in_=T2.t[:, :, 0:n], func=AF.Exp, scale=-0.5), T2.rg(), T2.rg())
        g.op("dve", lambda e: e.tensor_scalar(out=T1.t[:, :, 0:n], in0=T1.t[:, :, 0:n], scalar1=pcol(gain_row, 0), scalar2=float(scale),
                                              op0=ALU.mult, op1=ALU.mult), T1.rg() + pv.rg(), T1.rg())
        g.op("dve", lambda e: e.tensor_tensor(out=out_ap3, in0=T1.t[:, :, 0:n], in1=T2.t[:, :, 0:n], op=ALU.mult), T1.rg() + T2.rg(), out_regs)

    def raw_sink(n):
        def sink(mi, mw, pa, pregs):
            g.copy(g.ev_eng(), T1.t[:, mi, 0:n], pa, pregs, T1.rg(mi))
        return sink

    def shared_kv(sq_, s, t0, n, blk0):
        rmsnorm_mod(xT, n, gkv.t[:, s, :], lambda c: kvm.t[:, c, s:s + 1], out_bf=hbf)
        xs = xs_of(hbf, n)
        kind, bidx, T = sq_
        nb = (n + 127) // 128
        want_out = (kind == "s") or (t0 + n > T - KEEP)

        proj(I["w_kv"][:, 0:D], xs, n, raw_sink(n))
        headnorm_all(n, V("k_norm"), 1.0, yT.t[:, :, 0:n], yT.rg())
        for b in range(nb):
            nn = min(128, n - b * 128)
            slot = (blk0 + b) % NSLOT
            g.copy("act", Kring.t[:, :, slot * 128:slot * 128 + nn], yT.t[:, :, b * 128:b * 128 + nn], yT.rg(), Kring.rg(slot))
        for half in range(2):
            wsl = []
            for q4 in range(2):
                m0 = D + half * 512 + q4 * 256
                wsl.append(wslab(I["w_kv"][:, m0:m0 + 256].rearrange("(kc p) m -> p kc m", p=128)))
            for b in range(nb):
                nn = min(128, n - b * 128)
                slot = (blk0 + b) % NSLOT
                pb = PS[6 + (b + half) % 2]
                for q4 in range(2):
                    wb, wregs = wsl[q4]
                    g.mm([(pb.t[0:nn, q4 * 256:(q4 + 1) * 256], hbf.t[:, c, b * 128:b * 128 + nn], wb[:, c, :], c == 0, c == KC - 1)
                          for c in range(KC)], hbf.rg() + wregs, pb.rg(2 * q4, 2 * q4 + 2))
                g.copy("act", Vring.t[0:nn, slot, half * 512:(half + 1) * 512], pb.t[0:nn, :], pb.rg(), Vring.rg(slot))
                if want_out:
                    g.copy("dve", kvst.t[0:nn, b, half * 512:(half + 1) * 512], pb.t[0:nn, :], pb.rg(), kvst.rg(b))
        if want_out:
            if kind == "s":
                store_tok(yT, n, O["k_s"][0])
                g.dma(kvst, O["v_s"][0][0:n, :], kvst.t[0:n, 0, :], kvst.rg(0), [], is_out=True)
            else:
                r0 = t0 - (T - KEEP)
                store_tok(yT, n, O["k_p"][bidx][r0:r0 + n, :])
                for b in range(nb):
                    g.dma(kvst, O["v_p"][bidx][r0 + b * 128:r0 + (b + 1) * 128, :], kvst.t[:, b, :], kvst.rg(b), [], is_out=True)

    def attn(l, s, n, blk0, sq_):
        j = l - 2
        kind, bidx, T = sq_
        gmrow = gm.t[:, (l * 2 + 0) * 3 + s, :]
        rmsnorm_mod(xT, n, gmrow, lambda c: mod.t[:, l, 0 * 8 + c, s:s + 1], out_bf=hbf)

        proj(I["wb_q"][j], xs_of(hbf, n), n, raw_sink(n))
        headnorm_all(n, V("q_norm") + j, 0.125, qT.t[:, :, 0:n], qT.rg())
        nu = (n + 127) // 128
        units = [(h, u) for h in range(16) for u in range(nu)]
        info = {}

        def stage1(h, u):
            bb = biasb[h % 2]
            if u == 0:
                g.dma(bb, bb.t[:], BIASB.t[j * 16 + h].rearrange("p (t q) -> p t q", t=5), BIASB.rg(j * 16 + h), bb.rg())
            hp, half = h // 2, h % 2
            pb_ = half * 64
            nq = min(128, n - u * 128)
            qb = blk0 + u
            tiles = []
            for t in range(5):
                kb = qb - 4 + t
                if kb < 0:
                    continue
                tiles.append((t, kb % NSLOT, 128 if t < 4 else nq))
            pt = PT[(h * nu + u) % 2]
            SA, SB = (PS[0], PS[1]) if h % 2 == 0 else (PS[4], PS[5])

            def mmt(t, slot, nk, o):
                return [(o, Kring.t[pb_:pb_ + 64, hp, slot * 128:slot * 128 + nk], qT.t[pb_:pb_ + 64, hp, u * 128:u * 128 + nq], True, False),
                        (o, identb.t[0:nk, 0:nk], bb.t[0:nk, 4 - t, 0:nq], False, True)]
            full = [x for x in tiles if x[0] < 4]
            if full:
                items, rd = [], []
                for (t, slot, nk) in full:
                    items += mmt(t, slot, nk, SA.t[0:nk, t * 128:t * 128 + nq])
                    rd += Kring.rg(slot)
                g.mm(items, rd + qT.rg(hp) + bb.rg() + identb.rg(), SA.rg())
                t0_ = full[0][0]
                src = SA.t[:, t0_ * 128:512].rearrange("p (a b) -> p a b", a=4 - t0_)[:, :, 0:nq]
                g.op("act", lambda e: e.activation(out=pt.t[:, t0_:4, 0:nq], in_=src, func=AF.Exp), SA.rg(), pt.rg())
            (t, slot, nk) = tiles[-1]
            o = SB.t[0:nk, 0:nq]
            g.mm(mmt(t, slot, nk, o), Kring.rg(slot) + qT.rg(hp) + bb.rg() + identb.rg(), SB.rg())
            g.op("act", lambda e: e.activation(out=pt.t[0:nk, 4, 0:nq], in_=o, func=AF.Exp), SB.rg(), pt.rg())
            info[(h, u)] = (tiles, pt, nq)

        def stage2(h, u):
            tiles, pt, nq = info[(h, u)]
            hp, half = h // 2, h % 2
            pb_ = half * 64
            ob = PS[2 + half]
            oo = ob.t[:, u * 128:u * 128 + nq]
            dd = ob.t[:, 256 + u * 128:256 + u * 128 + nq]
            items = []
            for ti, (t, slot, nk) in enumerate(tiles):
                items.append((oo, Vring.t[0:nk, slot, hp * 128:(hp + 1) * 128], pt.t[0:nk, t, 0:nq], ti == 0, ti == len(tiles) - 1))
            for ti, (t, slot, nk) in enumerate(tiles):
                items.append((dd, ones1.t[0:nk, :], pt.t[0:nk, t, 0:nq], ti == 0, ti == len(tiles) - 1))
            vr = []
            for (t, slot, nk) in tiles:
                vr += Vring.rg(slot)
            g.mm(items, vr + pt.rg() + ones1.rg(), ob.rg())
            rc = tmp()
            g.op("dve", lambda e: e.reciprocal(out=rc.t[pb_:pb_ + 64, 0:nq], in_=dd[pb_:pb_ + 64, :]), ob.rg(), rc.rg())
            g.op("dve", lambda e: e.tensor_tensor(out=oT.t[pb_:pb_ + 64, hp, u * 128:u * 128 + nq], in0=oo[pb_:pb_ + 64, :],
                                                  in1=rc.t[pb_:pb_ + 64, 0:nq], op=ALU.mult), ob.rg() + rc.rg(), oT.rg(hp))

        for i, (h, u) in enumerate(units):
            stage1(h, u)
            if i > 0:
                stage2(*units[i - 1])
        stage2(*units[-1])
        proj(I["wb_o"][j], xs_of(oT, n), n, resid_sink(l, s, n))

    def state_load(l):
        g.dma(sst, sst.t[:], I["state_wkv"][l, 0].rearrange("h i j -> i h j"), [], sst.rg())
        for half in range(2):
            pb = PS[6 + half]
            items = []
            for hp in range(KC):
                h = hp * 2 + half
                items.append((pb.t[0:64, hp * 64:(hp + 1) * 64], sst.t[0:64, h, :], ident.t[0:64, 0:64]))
            g.tr(items, sst.rg() + ident.rg(), pb.rg())
            g.op("dve", lambda e: e.tensor_copy(out=Sf[l].t[half * 64:half * 64 + 64, :, :],
                                                in_=pb.t[0:64, :].rearrange("p (a b) -> p a b", a=KC)), pb.rg(), Sf[l].rg())
            g.op("dve", lambda e: e.tensor_copy(out=Sb[l].t[half * 64:half * 64 + 64, :, half * 64:half * 64 + 64],
                                                in_=pb.t[0:64, :].rearrange("p (a b) -> p a b", a=KC)), pb.rg(), Sb[l].rg())

    def state_store(l, dst):
        for half in range(2):
            pb = PS[6 + half]
            items = []
            for hp in range(KC):
                items.append((pb.t[0:64, hp * 64:(hp + 1) * 64], Sf[l].t[half * 64:half * 64 + 64, hp, :], ident.t[half * 64:half * 64 + 64, half * 64:half * 64 + 64]))
            g.tr(items, Sf[l].rg() + ident.rg(), pb.rg())
            g.op("dve", lambda e: e.tensor_copy(out=sst.t[:, :, :].rearrange("p (a two) b -> p a two b", two=2)[:, :, half, :],
                                                in_=pb.t[0:64, :].rearrange("p (a b) -> p a b", a=KC)), pb.rg(), sst.rg())
        g.dma(sst, dst.rearrange("h i j -> i h j"), sst.t[:], sst.rg(), [], is_out=True)

    def shift_store(l, dst_row):
        pb = PS[6]
        g.tr([(pb.t[0:KC, 0:128], carry[l].t[:, :], ident.t[:, :])], carry[l].rg() + ident.rg(), pb.rg(0, 1))
        st_ = tmp()
        g.copy("dve", st_.t[0:KC, 0:128], pb.t[0:KC, 0:128], pb.rg(0, 1), st_.rg())
        g.dma(st_, dst_row.rearrange("(c p) -> c p", p=128), st_.t[0:KC, 0:128], st_.rg(), [], is_out=True)

    seqs = [("p", 0, TP), ("p", 1, TP), ("s", 0, TS)]
    for s, sq_ in enumerate(seqs):
        kind, bidx, T = sq_
        for l in range(2):
            if kind == "p":
                g.op("dve", lambda e: e.memset(carry[l].t[:], 0.0), [], carry[l].rg())
                g.op("dve", lambda e: e.memset(Sf[l].t[:], 0.0), [], Sf[l].rg())
                g.op("dve", lambda e: e.memset(Sb[l].t[:], 0.0), [], Sb[l].rg())
            else:
                r = V("state_shift") + l
                g.op("dve", lambda e: e.tensor_copy(out=carry[l].t[:, :], in_=pv.t[:, r, :]), pv.rg(), carry[l].rg())
                state_load(l)
        if kind == "s":
            for slot in range(4):
                load_xT(I["cache_k"][0][slot * 128:(slot + 1) * 128, :], 128, yT)
                for c in range(KC):
                    g.op("pool", lambda e: e.tensor_copy(out=Kring.t[:, c, slot * 128:(slot + 1) * 128], in_=yT.t[:, c, 0:128]),
                         yT.rg(c), Kring.rg(slot))
                g.dma(kvst, kvst.t[:, 0, :], I["cache_v"][0][slot * 128:(slot + 1) * 128, :], [], kvst.rg(0))
                g.op("pool", lambda e: e.tensor_copy(out=Vring.t[:, slot, :], in_=kvst.t[:, 0, :]), kvst.rg(0), Vring.rg(slot))
        ntile = (T + NT - 1) // NT
        for ti in range(ntile):
            t0 = ti * NT
            n = min(NT, T - t0)
            blk0 = (t0 // 128) if kind == "p" else 4
            ck(31)
            xsrc = I["x_prompt"][bidx] if kind == "p" else I["x_sample"][0]
            load_xT(xsrc[t0:t0 + n, :], n, xT)
            ck(4)
            for l in range(2):
                rwkv(l, s, n, ti == 0)
                ck(5 + 2 * l)
                ffn(l, s, n)
                ck(6 + 2 * l)
            shared_kv(sq_, s, t0, n, blk0)
            ck(9)
            for l in range(2, 4):
                attn(l, s, n, blk0, sq_)
                ck(10 + 2 * (l - 2))
                ffn(l, s, n)
                ck(11 + 2 * (l - 2))
            ydst = O["y_prompt"][bidx] if kind == "p" else O["y_sample"][0]
            store_tok(xT, n, ydst[t0:t0 + n, :])
        for l in range(2):
            if kind == "p":
                shift_store(l, O["shift_p"][l, bidx])
                state_store(l, O["wkv_p"][l, bidx])
            else:
                shift_store(l, O["shift_s"][l, 0])
                state_store(l, O["wkv_s"][l, 0])


_CACHE = {}


def _run(inputs, TP, TS, ncores):
    key = (TP, TS)
    if key not in _CACHE:
        _CACHE[key] = build(TP, TS)
    nc, st = _CACHE[key]
    f = lambda a: np.ascontiguousarray(np.asarray(a, dtype=np.float32))
    in_maps = []
    for c in range(ncores):
        m = {}
        for k, v in inputs.items():
            v = f(v)
            if k == "x_prompt":
                m[k] = f(v[2 * c:2 * c + 2])
            elif k == "c_prompt":
                m[k] = f(v[2 * c:2 * c + 2])
            elif k in ("x_sample", "c_sample"):
                m[k] = f(v[c:c + 1])
            elif k in ("state_shift", "state_wkv"):
                m[k] = f(v[:, c:c + 1])
            elif k in ("cache_k", "cache_v"):
                m[k] = f(v[c:c + 1].reshape(1, 512, D))
            elif k == "rw_rk":
                m[k] = f(v.reshape(2, D))
            else:
                m[k] = v
        in_maps.append(m)
    res = run_bass_kernel_spmd(nc, in_maps, core_ids=list(range(ncores)))
    R = res.results
    cat = lambda k, ax: np.concatenate([R[c][k] for c in range(ncores)], axis=ax)
    KEEP = min(512, TP)
    y_prompt = cat("y_prompt", 0)
    y_sample = cat("y_sample", 0)
    shift_p = cat("shift_p", 1)
    wkv_p = cat("wkv_p", 1)
    k_p = cat("k_p", 0).reshape(2 * ncores, KEEP, 16, 64)
    v_p = cat("v_p", 0).reshape(2 * ncores, KEEP, 16, 64)
    shift_s = cat("shift_s", 1)
    wkv_s = cat("wkv_s", 1)
    k_s = cat("k_s", 0).reshape(ncores, TS, 16, 64)
    v_s = cat("v_s", 0).reshape(ncores, TS, 16, 64)
    return (y_prompt, y_sample, shift_p, wkv_p, k_p, v_p, shift_s, wkv_s, k_s, v_s)


def kernel(**inputs):
    TP = inputs["x_prompt"].shape[1]
    TS = inputs["x_sample"].shape[1]
    return _run(inputs, TP, TS, 8)
```
